# Optimizing a Trainium2 kernel written in Bass

```python
import math
import jax, jax.numpy as jnp
from jax import lax
import numpy as np

D_MODEL = 1024
BATCH = 4
SEQ = 4096
DEPTH = 1

DA_HEADS = 8
DA_HEAD_DIM = D_MODEL // DA_HEADS // 2
DA_V_DIM = 2 * DA_HEAD_DIM
DA_QK_WIDTH = DA_HEADS * 2 * DA_HEAD_DIM
DA_V_WIDTH = DA_HEADS * DA_V_DIM
Q_BLOCK = 128
ML_HEADS = 4
ML_V_DIM = D_MODEL // ML_HEADS
ML_QK_DIM = ML_V_DIM // 2
ML_QK_WIDTH = ML_HEADS * ML_QK_DIM
ML_V_WIDTH = ML_HEADS * ML_V_DIM
ML_CHUNK = 64
CONV_WIDTH = 4
N_BRANCHES = 2
D_FF = 4 * D_MODEL
EPS = 1e-6

SPLIT_SIZES = (DA_QK_WIDTH, DA_QK_WIDTH, DA_V_WIDTH,
               ML_QK_WIDTH, ML_QK_WIDTH, ML_V_WIDTH, 2 * ML_HEADS, ML_V_WIDTH,
               N_BRANCHES * D_MODEL)
D_IN = int(sum(SPLIT_SIZES))
SPLIT_POINTS = [int(s) for s in np.cumsum(SPLIT_SIZES)[:-1]]

kernel_name = "hybrid_gated_diffattn_mlstm_block"


def rms_norm(x, g):
    xf = x.astype(jnp.float32)
    y = xf * lax.rsqrt(jnp.mean(xf * xf, axis=-1, keepdims=True) + EPS)
    return (y * g.astype(jnp.float32)).astype(x.dtype)


def head_rms(x):
    return x * lax.rsqrt(jnp.mean(x * x, axis=-1, keepdims=True) + EPS)


def alibi_slopes(n_heads):
    return 2.0 ** (-8.0 * jnp.arange(1, n_heads + 1, dtype=jnp.float32) / n_heads)


def causal_conv(u, w, b):
    S = u.shape[1]
    up = jnp.pad(u, ((0, 0), (CONV_WIDTH - 1, 0), (0, 0)))
    out = sum(up[:, j:j + S] * w[j] for j in range(CONV_WIDTH))
    return out + b


def diff_attention(q, k, v, lam, lam_init, norm_g):
    B, S, _ = q.shape
    f32 = jnp.float32
    q = q.astype(f32).reshape(B, S, DA_HEADS, 2, DA_HEAD_DIM).transpose(0, 2, 3, 1, 4) * (DA_HEAD_DIM ** -0.5)
    k = k.astype(f32).reshape(B, S, DA_HEADS, 2, DA_HEAD_DIM).transpose(0, 2, 3, 1, 4)
    vf = v.astype(f32).reshape(B, S, DA_HEADS, DA_V_DIM).transpose(0, 2, 1, 3)
    lamf = lam.astype(f32)
    lam_full = jnp.exp(jnp.sum(lamf[0] * lamf[1])) - jnp.exp(jnp.sum(lamf[2] * lamf[3])) + lam_init
    slopes = alibi_slopes(DA_HEADS)
    key_pos = jnp.arange(S)
    n_blocks = S // Q_BLOCK
    q_blocks = q.reshape(B, DA_HEADS, 2, n_blocks, Q_BLOCK, DA_HEAD_DIM).transpose(3, 0, 1, 2, 4, 5)
    starts = jnp.arange(n_blocks) * Q_BLOCK

    def one_block(args):
        qb, start = args
        qpos = start + jnp.arange(Q_BLOCK)
        dist = qpos[:, None] - key_pos[None, :]
        causal = dist >= 0
        bias = -slopes[:, None, None] * dist.astype(f32)
        s = jnp.einsum('bhmqd,bhmkd->bhmqk', qb, k) + bias[None, :, None]
        s = jnp.where(causal, s, -jnp.inf)
        p = jax.nn.softmax(s, axis=-1)
        a = p[:, :, 0] - lam_full * p[:, :, 1]
        return jnp.einsum('bhqk,bhkd->bhqd', a, vf)

    o = lax.map(one_block, (q_blocks, starts))
    o = o.transpose(1, 0, 3, 2, 4).reshape(B, S, DA_HEADS, DA_V_DIM)
    o = head_rms(o) * (1.0 - lam_init)
    o = o.reshape(B, S, DA_V_WIDTH) * norm_g.astype(f32)
    return o.astype(v.dtype)


def mlstm(q, k, v, i_pre, f_pre, o_pre, norm_g):
    B, S, _ = q.shape
    f32 = jnp.float32
    nc = S // ML_CHUNK

    def heads(t, d):
        return t.astype(f32).reshape(B, nc, ML_CHUNK, ML_HEADS, d).transpose(1, 0, 3, 2, 4)

    def gates(t):
        return t.astype(f32).reshape(B, nc, ML_CHUNK, ML_HEADS).transpose(1, 0, 3, 2)

    qc = heads(q, ML_QK_DIM) * (ML_QK_DIM ** -0.5)
    kc = heads(k, ML_QK_DIM)
    vc = heads(v, ML_V_DIM)
    ic = gates(i_pre)
    lfc = gates(jax.nn.log_sigmoid(f_pre.astype(f32)))
    causal = jnp.tril(jnp.ones((ML_CHUNK, ML_CHUNK), dtype=bool))

    def step(carry, xs):
        C, n, m = carry
        q_, k_, v_, ig, lf = xs
        b = jnp.cumsum(lf, axis=-1)
        g = b[..., -1]
        logd = b[..., :, None] - b[..., None, :] + ig[..., None, :]
        logd = jnp.where(causal, logd, -jnp.inf)
        inter = b + m[..., None]
        m_t = jnp.maximum(inter, jnp.max(logd, axis=-1))
        sc = jnp.einsum('bhtd,bhsd->bhts', q_, k_) * jnp.exp(logd - m_t[..., None])
        w_inter = jnp.exp(inter - m_t)
        num = w_inter[..., None] * jnp.einsum('bhtd,bhde->bhte', q_, C) + jnp.einsum('bhts,bhse->bhte', sc, v_)
        den = w_inter * jnp.einsum('bhtd,bhd->bht', q_, n) + jnp.sum(sc, axis=-1)
        h = num / jnp.maximum(jnp.abs(den), jnp.exp(-m_t))[..., None]
        log_w = g[..., None] - b + ig
        m_new = jnp.maximum(g + m, jnp.max(log_w, axis=-1))
        w_s = jnp.exp(log_w - m_new[..., None])
        decay = jnp.exp(g + m - m_new)
        C_new = decay[..., None, None] * C + jnp.einsum('bhs,bhsd,bhse->bhde', w_s, k_, v_)
        n_new = decay[..., None] * n + jnp.einsum('bhs,bhsd->bhd', w_s, k_)
        return (C_new, n_new, m_new), h

    init = (jnp.zeros((B, ML_HEADS, ML_QK_DIM, ML_V_DIM), f32),
            jnp.zeros((B, ML_HEADS, ML_QK_DIM), f32),
            jnp.zeros((B, ML_HEADS), f32))
    _, hs = lax.scan(step, init, (qc, kc, vc, ic, lfc))
    h = hs.transpose(1, 0, 3, 2, 4).reshape(B, S, ML_HEADS, ML_V_DIM)
    h = head_rms(h).reshape(B, S, ML_V_WIDTH) * norm_g.astype(f32)
    h = jax.nn.sigmoid(o_pre.astype(f32)) * h
    return h.astype(v.dtype)


def setup_inputs(seed: int = 0) -> dict:
    key = jax.random.key(seed)
    ks = jax.random.split(key, 20)
    nrm = jax.random.normal
    L = DEPTH
    f_bias = jnp.linspace(3.0, 6.0, ML_HEADS)[None, :] + 0.01 * nrm(ks[3], (L, ML_HEADS))
    i_bias = 0.1 * nrm(ks[4], (L, ML_HEADS))
    return {
        "x": nrm(ks[0], (BATCH, SEQ, D_MODEL), jnp.float32),
        "norm_mix_g": 1.0 + 0.02 * nrm(ks[1], (L, D_MODEL)),
        "w_in": nrm(ks[2], (L, D_MODEL, D_IN)) * D_MODEL ** -0.5,
        "b_gates": jnp.concatenate([i_bias, f_bias], axis=-1),
        "conv_w": nrm(ks[5], (L, CONV_WIDTH, 2 * ML_QK_WIDTH)) * CONV_WIDTH ** -0.5,
        "conv_b": 0.01 * nrm(ks[6], (L, 2 * ML_QK_WIDTH)),
        "lam": 0.1 * nrm(ks[7], (L, 4, DA_HEAD_DIM)),
        "da_norm_g": 1.0 + 0.02 * nrm(ks[8], (L, DA_V_WIDTH)),
        "ml_norm_g": 1.0 + 0.02 * nrm(ks[9], (L, ML_V_WIDTH)),
        "b_merge": 0.01 * nrm(ks[10], (L, N_BRANCHES * D_MODEL)),
        "w_branch_a": nrm(ks[11], (L, DA_V_WIDTH, D_MODEL)) * DA_V_WIDTH ** -0.5,
        "w_branch_m": nrm(ks[12], (L, ML_V_WIDTH, D_MODEL)) * ML_V_WIDTH ** -0.5,
        "w_out": nrm(ks[13], (L, D_MODEL, D_MODEL)) * D_MODEL ** -0.5,
        "norm_mlp_g": 1.0 + 0.02 * nrm(ks[14], (L, D_MODEL)),
        "w_ff1": nrm(ks[15], (L, D_MODEL, D_FF)) * D_MODEL ** -0.5,
        "w_ff2": nrm(ks[16], (L, D_FF, D_MODEL)) * D_FF ** -0.5,
        "norm_final_g": 1.0 + 0.02 * nrm(ks[17], (D_MODEL,)),
    }


def reference(x, norm_mix_g, w_in, b_gates, conv_w, conv_b, lam, da_norm_g, ml_norm_g,
              b_merge, w_branch_a, w_branch_m, w_out, norm_mlp_g, w_ff1, w_ff2, norm_final_g):
    for l in range(DEPTH):
        lam_init = 0.8 - 0.6 * math.exp(-0.3 * l)
        h = rms_norm(x, norm_mix_g[l])
        proj = h @ w_in[l]
        da_q, da_k, da_v, ml_q, ml_k, ml_v, ml_if, ml_o, mg = jnp.split(proj, SPLIT_POINTS, axis=-1)
        a_out = diff_attention(da_q, da_k, da_v, lam[l], lam_init, da_norm_g[l])
        qk = jax.nn.silu(causal_conv(jnp.concatenate([ml_q, ml_k], axis=-1), conv_w[l], conv_b[l]))
        ml_qc, ml_kc = jnp.split(qk, [ML_QK_WIDTH], axis=-1)
        if_pre = ml_if + b_gates[l]
        m_out = mlstm(ml_qc, ml_kc, ml_v, if_pre[..., :ML_HEADS], if_pre[..., ML_HEADS:], ml_o, ml_norm_g[l])
        gate = jax.nn.sigmoid(mg + b_merge[l])
        g_a, g_m = jnp.split(gate, [D_MODEL], axis=-1)
        merged = g_a * (a_out @ w_branch_a[l]) + g_m * (m_out @ w_branch_m[l])
        x = x + merged @ w_out[l]
        hm = rms_norm(x, norm_mlp_g[l])
        x = x + jnp.square(jax.nn.relu(hm @ w_ff1[l])) @ w_ff2[l]
    return rms_norm(x, norm_final_g)
```

```python
import numpy as np
import ml_dtypes
import concourse.bass as bass
import concourse.mybir as mybir
from concourse.bass_utils import run_bass_kernel_spmd

F32 = mybir.dt.float32
BF16 = mybir.dt.bfloat16
AF = mybir.ActivationFunctionType
ALU = mybir.AluOpType

S = 4096
D = 1024
NT = 32
NPAIR = 16
DIN = 8200
EPS = 1e-6
LAM_INIT = 0.2
NEG = -30000.0
DBG = {}
FLAGS = {'mask128': True, 'b2late': True, 'qkint': True, 'trimmm': False, 'trim': True, 'dveonly': True, 'dbl': True, 'pb': True}


class Prog:
    def __init__(self, nc):
        self.nc = nc
        self.engs = {'pe': nc.tensor, 'dve': nc.vector, 'act': nc.scalar,
                     'pool': nc.gpsimd, 'sp': nc.sync}
        self.sem = {k: nc.alloc_semaphore(name='s_' + k) for k in self.engs}
        self.bar = nc.alloc_semaphore(name='s_bar')
        self.nbar = 0
        self.cnt = {k: 0 for k in self.engs}
        self.dsem = {}
        self.dcnt = {}
        self.waited = {}
        self.last_w = {}
        self.readers = {}

    def _semh(self, key):
        return self.sem[key] if key in self.sem else self.dsem[key]

    def _wait(self, eng, deps):
        best = {}
        for (k, v) in deps:
            if k == 'pe' and eng == 'pe':
                continue
            if v > best.get(k, 0):
                best[k] = v
        for k, v in best.items():
            if self.waited.get((eng, k), 0) >= v:
                continue
            self.engs[eng].wait_ge(self._semh(k), v)
            self.waited[(eng, k)] = v

    def _deps(self, reads, writes):
        deps = set()
        for b in reads:
            if b in self.last_w:
                deps.add(self.last_w[b])
        for b in writes:
            if b in self.last_w:
                deps.add(self.last_w[b])
            deps.update(self.readers.get(b, ()))
        return deps

    def _commit(self, me, reads, writes):
        for b in reads:
            self.readers.setdefault(b, []).append(me)
        for b in writes:
            self.last_w[b] = me
            self.readers[b] = []

    def op(self, eng, fn, reads=(), writes=()):
        self._wait(eng, self._deps(reads, writes))
        inst = fn(self.engs[eng])
        self.cnt[eng] += 1
        inst.then_inc(self.sem[eng], 1)
        self._commit((eng, self.cnt[eng]), reads, writes)

    def dma(self, q, out, in_, reads=(), writes=(), dsem=None):
        if dsem not in self.dsem:
            self.dsem[dsem] = self.nc.alloc_semaphore(name='d_' + str(len(self.dsem)))
            self.dcnt[dsem] = 0
        self._wait(q, self._deps(reads, writes))
        inst = self.engs[q].dma_start(out=out, in_=in_)
        self.dcnt[dsem] += 16
        inst.then_inc(self.dsem[dsem], 16)
        self._commit((dsem, self.dcnt[dsem]), reads, writes)

    def barrier(self):
        sp = self.engs['sp']
        for k in self.engs:
            if k != 'sp' and self.cnt[k]:
                sp.wait_ge(self.sem[k], self.cnt[k])
        for k, v in self.dcnt.items():
            sp.wait_ge(self.dsem[k], v)
        self.nbar += 1
        sp.sem_inc(self.bar, 1)
        for k in self.engs:
            if k != 'sp':
                self.engs[k].wait_ge(self.bar, self.nbar)
        self.last_w = {}
        self.readers = {}

    def finish(self):
        sp = self.engs['sp']
        for k in self.engs:
            if k != 'sp' and self.cnt[k]:
                sp.wait_ge(self.sem[k], self.cnt[k])
        for k, v in self.dcnt.items():
            sp.wait_ge(self.dsem[k], v)


class Arena:
    def __init__(self, big, base, limit):
        self.big = big
        self.off = base
        self.limit = limit

    def take(self, shape, dt):
        n = 1
        for s in shape[1:]:
            n *= s
        esz = 4 if dt == F32 else 2
        nbytes = n * esz
        off = self.off
        self.off += (nbytes + 63) // 64 * 64
        assert self.off <= self.limit, (self.off, self.limit)
        v = self.big[:, off // 2: off // 2 + nbytes // 2]
        if dt == F32:
            v = v.bitcast(F32)
        if len(shape) == 3:
            v = v.rearrange("p (a b) -> p a b", a=shape[1])
        elif len(shape) == 4:
            v = v.rearrange("p (a b c) -> p a b c", a=shape[1], b=shape[2])
        return v


PP_CW = 0
PP_CB = 32
PP_BM = 40
PP_KPOS = 56
PP_Q128 = 88
PP_Q256 = 104
PP_Q512 = 112
PP_FE = 116
PP_FO = 117
PP_N = 128

VB_MIX = 0
VB_DAG = 1024
VB_MLG = 2048
VB_MLP = 3072
VB_FIN = 4096
VB_BG = 5120
VB_LAM = 5128
VB_N = 5384

C_Q, C_K, C_V = 0, 1024, 2048
C_MQ, C_MK, C_MV, C_IF, C_MO, C_MG = 3072, 3584, 4096, 5120, 5128, 6152


def build(phases=5):
    nc = bass.Bass("TRN2", target_bir_lowering=False)
    xl = nc.dram_tensor("xl", [S, D], F32, kind="ExternalInput").ap()
    w_in = nc.dram_tensor("w_in", [D, DIN], F32, kind="ExternalInput").ap()
    w_a = nc.dram_tensor("w_a", [D, D], F32, kind="ExternalInput").ap()
    w_m = nc.dram_tensor("w_m", [D, D], F32, kind="ExternalInput").ap()
    w_o = nc.dram_tensor("w_o", [D, D], F32, kind="ExternalInput").ap()
    w_1 = nc.dram_tensor("w_1", [D, 4 * D], F32, kind="ExternalInput").ap()
    w_2 = nc.dram_tensor("w_2", [4 * D, D], F32, kind="ExternalInput").ap()
    vb = nc.dram_tensor("vb", [1, VB_N], F32, kind="ExternalInput").ap()
    ppd = nc.dram_tensor("pp", [128, PP_N], F32, kind="ExternalInput").ap()
    cst = nc.dram_tensor("cst", [128, 384], F32, kind="ExternalInput").ap()
    mkd = nc.dram_tensor("mk", [128, 8 * 512], F32, kind="ExternalInput").ap()
    kaugd = nc.dram_tensor("kaug", [4, S], F32, kind="ExternalInput").ap()
    qaugd = nc.dram_tensor("qaug", [8, 4, S // 2], F32, kind="ExternalInput").ap()
    out = nc.dram_tensor("out", [S // 2, D], F32, kind="ExternalOutput").ap()
    dbg_out = {}
    for name, shape in DBG.items():
        dbg_out[name] = nc.dram_tensor(name, list(shape), F32, kind="ExternalOutput").ap()

    P = Prog(nc)
    import contextlib
    with contextlib.ExitStack() as es:
        TOT = 205 * 1024
        big = es.enter_context(nc.sbuf_tensor("big", [128, TOT // 2], BF16))
        psum = es.enter_context(nc.psum_tensor("psum", [128, 8, 512], F32))

        def bank(i):
            return psum[:, i, :]

        def bankb(i):
            return psum[:, i, :].bitcast(BF16)

        CA = Arena(big, 0, 12800)
        identb = CA.take([128, 128], BF16)
        trib = CA.take([128, 128], BF16)
        trif = CA.take([128, 128], F32)
        onesf = CA.take([128, 128], F32)
        pp = CA.take([128, PP_N], F32)
        nqh = CA.take([128, 8, 28], F32)
        lamb = CA.take([128, 256], F32)
        small = CA.take([128, 64], F32)
        gbcA = CA.take([128, 1024], F32)
        gbcB = CA.take([128, 1024], F32)
        R1 = 12800
        R2 = R1 + 65536
        R3 = R2 + 32768
        T0 = R3 + 32768
        hT = Arena(big, R1, R2).take([128, 8, S], BF16)
        aT = Arena(big, R2, R3).take([128, 8, S // 2], BF16)
        mT = Arena(big, R3, T0).take([128, 8, S // 2], BF16)
        hT5 = hT.rearrange("p k (q e t) -> p k q e t", q=NPAIR, e=2)

        def dbg_store(name, src_ap, key, n=0):
            if name in dbg_out:
                P.dma('sp', dbg_out[name], src_ap, reads=[key], dsem='dbg_' + name)

        P.dma('pool', identb, cst[:, 0:128], writes=['identb'], dsem='c0')
        P.dma('pool', trib, cst[:, 128:256], writes=['trib'], dsem='c1')
        P.dma('sp', trif, cst[:, 128:256], writes=['trif'], dsem='c2')
        P.dma('sp', onesf, cst[:, 256:384], writes=['onesf'], dsem='c3')
        P.dma('sp', pp, ppd, writes=['pp'], dsem='c4')
        P.dma('sp', lamb, vb[:, VB_LAM:VB_LAM + 256].partition_broadcast(128), writes=['lamb'], dsem='c5')
        P.dma('sp', gbcA, vb[:, VB_MIX:VB_MIX + 1024].partition_broadcast(128), writes=['gbcA'], dsem='gA')
        P.op('dve', lambda e: e.tensor_tensor(out=lamb[:, 0:64], in0=lamb[:, 0:64], in1=lamb[:, 64:128], op=ALU.mult),
             reads=['lamb'], writes=['lamb'])
        P.op('dve', lambda e: e.tensor_tensor(out=lamb[:, 128:192], in0=lamb[:, 128:192], in1=lamb[:, 192:256], op=ALU.mult),
             reads=['lamb'], writes=['lamb'])
        P.op('dve', lambda e: e.tensor_reduce(out=small[:, 0:1], in_=lamb[:, 0:64], axis=mybir.AxisListType.X, op=ALU.add),
             reads=['lamb'], writes=['sm0'])
        P.op('dve', lambda e: e.tensor_reduce(out=small[:, 1:2], in_=lamb[:, 128:192], axis=mybir.AxisListType.X, op=ALU.add),
             reads=['lamb'], writes=['sm1'])
        P.op('act', lambda e: e.activation(out=small[:, 3:5], in_=small[:, 0:2], func=AF.Exp), reads=['sm0', 'sm1'], writes=['sm34'])
        P.op('dve', lambda e: e.scalar_tensor_tensor(out=small[:, 2:3], in0=small[:, 4:5], scalar=-LAM_INIT, in1=small[:, 3:4],
                                                     op0=ALU.add, op1=ALU.subtract), reads=['sm34'], writes=['neglam'])
        for h in range(8):
            P.op('dve', lambda e, h=h: e.tensor_scalar(out=nqh[:, h, :], in0=pp[:, PP_Q128:PP_Q128 + 28],
                                                       scalar1=-(2.0 ** -(h + 1)), scalar2=None, op0=ALU.mult),
                 reads=['pp'], writes=['nqh'])

        TA = Arena(big, T0, TOT)
        xin = [TA.take([128, 4, 1024], F32) for _ in range(2)]
        hb = [TA.take([128, 1024], BF16) for _ in range(4)]
        junk = TA.take([128, 1024], BF16)
        sc0 = TA.take([128, 16 * 8], F32)
        xl4 = xl.rearrange("(g t p) d -> g p t d", t=4, p=128)
        def p0_X(g):
            xs = g % 2
            xk = 'xin%d' % xs
            P.dma('sp' if xs == 0 else 'pool', xin[xs], xl4[g], writes=[xk], dsem=xk)
            c0 = 16 * g
            for tt in range(4):
                P.op('act', lambda e, tt=tt: e.activation(out=junk, in_=xin[xs][:, tt, :], func=AF.Square,
                                                          accum_out=sc0[:, c0 + tt:c0 + tt + 1]),
                     reads=[xk], writes=['junk', 'sc0a_%d_%d' % (g, tt)])
            P.op('act', lambda e: e.activation(out=sc0[:, c0 + 4:c0 + 8], in_=sc0[:, c0:c0 + 4], func=AF.Ln, scale=1.0 / D, bias=EPS),
                 reads=['sc0a_%d_%d' % (g, tt) for tt in range(4)], writes=['sc0b_%d' % g])
            P.op('act', lambda e: e.activation(out=sc0[:, c0 + 8:c0 + 12], in_=sc0[:, c0 + 4:c0 + 8], func=AF.Exp, scale=-0.5),
                 reads=['sc0b_%d' % g], writes=['sc0c_%d' % g])

        def p0_Y(g):
            xs = g % 2
            xk = 'xin%d' % xs
            c0 = 16 * g
            for tt in range(4):
                P.op('dve', lambda e, tt=tt: e.scalar_tensor_tensor(
                    out=hb[tt], in0=xin[xs][:, tt, :], scalar=sc0[:, c0 + 8 + tt:c0 + 9 + tt], in1=gbcA, op0=ALU.mult, op1=ALU.mult),
                    reads=[xk, 'sc0c_%d' % g, 'gbcA'], writes=['hb%d' % tt])
            for tt in range(4):
                bk = 4 * xs + tt

                def tr(e, tt=tt, bk=bk):
                    for kc in range(8):
                        ins = e.transpose(out=bankb(bk)[:, kc * 128:(kc + 1) * 128], in_=hb[tt][:, kc * 128:(kc + 1) * 128],
                                          identity=identb)
                    return ins
                P.op('pe', tr, reads=['hb%d' % tt, 'identb'], writes=['ps%d' % bk])
            for tt in range(4):
                bk = 4 * xs + tt
                i = g * 4 + tt
                if tt % 2:
                    P.op('act', lambda e, bk=bk, i=i: e.copy(out=hT[:, :, i * 128:(i + 1) * 128],
                                                            in_=bankb(bk).rearrange("p (k t) -> p k t", k=8)),
                         reads=['ps%d' % bk], writes=['hT'])
                else:
                    P.op('dve', lambda e, bk=bk, i=i: e.tensor_copy(out=hT[:, :, i * 128:(i + 1) * 128],
                                                                   in_=bankb(bk).rearrange("p (k t) -> p k t", k=8)),
                         reads=['ps%d' % bk], writes=['hT'])

        p0_X(0)
        for g in range(8):
            if g + 1 < 8:
                p0_X(g + 1)
            p0_Y(g)
        if 'd_hT' in dbg_out:
            dtmp = TA.take([128, 1024], F32)
            P.op('dve', lambda e: e.tensor_copy(out=dtmp.rearrange("p (k t) -> p k t", k=8), in_=hT[:, :, 0:128]), reads=['hT'], writes=['dtmp'])
            dbg_store('d_hT', dtmp, 'dtmp')
        P.barrier()
        if phases <= 0:
            P.finish()
            return nc

        import math

        def wslice(col0, n):
            return w_in[:, col0:col0 + n].rearrange("(kc p) f -> p kc f", p=128)

        fe = pp[:, PP_FE:PP_FE + 1]
        fo = pp[:, PP_FO:PP_FO + 1]
        MA = Arena(big, R2, R3)
        QTm2 = [MA.take([128, S], BF16) for _ in range(2)]
        KTm2 = [MA.take([128, S], BF16) for _ in range(2)]
        TA = Arena(big, T0, TOT)
        Uh = TA.take([128, NT, 131], F32)
        Uh5 = Uh.rearrange("p (q e) t -> p q e t", e=2)
        Wqk = [TA.take([128, 8, 128], BF16) for _ in range(2)]
        cacc2 = [TA.take([128, 4, 128], F32) for _ in range(2)]
        stt = [TA.take([128, 4, 128], F32) for _ in range(2)]
        Wv = [TA.take([128, 8, 256], BF16) for _ in range(2)]
        Wo = [TA.take([128, 8, 256], BF16) for _ in range(2)]
        Wif = TA.take([128, 8, 8], BF16)
        pre = TA.take([128, NT, 8], F32)
        SPt_f = TA.take([128, NT * 4], F32)
        SPp_f = TA.take([128, NT * 4], F32)
        As_f = TA.take([128, NT * 4], F32)
        Ct_f = TA.take([128, NT * 4], F32)
        Gd_f = TA.take([128, NPAIR * 4], F32)
        tmpg_f = TA.take([128, NT * 4], F32)
        SPs_f = TA.take([128, NPAIR * 4], F32)
        v3 = lambda a: a.rearrange("p (i h) -> p i h", h=4)
        v4 = lambda a: a.rearrange("p (q e h) -> p q e h", e=2, h=4)
        SPt, SPp, As, Ct, Gd, tmpg, SPs = v3(SPt_f), v3(SPp_f), v3(As_f), v3(Ct_f), v3(Gd_f), v3(tmpg_f), v3(SPs_f)
        bg = TA.take([128, 8], F32)
        Va = [TA.take([128, 2, 258], BF16) for _ in range(2)]
        Km = [TA.take([128, 2, 128], BF16) for _ in range(2)]
        MTb = [TA.take([128, 2, 128], BF16) for _ in range(2)]
        E = TA.take([128, 258], F32)
        Cbf = [TA.take([128, 258], BF16) for _ in range(2)]
        hpre = [TA.take([128, 256], F32) for _ in range(2)]
        og = [TA.take([128, 256], F32) for _ in range(2)]
        hn = TA.take([128, 256], F32)
        mtok = [TA.take([128, 256], BF16) for _ in range(2)]
        junkm = TA.take([128, 256], BF16)
        scm = [TA.take([128, 16], F32) for _ in range(2)]

        P.dma('pool', Wif, wslice(C_IF, 8), writes=['Wif'], dsem='wif')
        P.dma('sp', bg, vb[:, VB_BG:VB_BG + 8].partition_broadcast(128), writes=['bg'], dsem='bg')
        P.dma('sp', gbcB, vb[:, VB_MLG:VB_MLG + 1024].partition_broadcast(128), writes=['gbcB'], dsem='gB')

        def gmm(e):
            for i in range(NT):
                for kc in range(8):
                    ins = e.matmul(bank(0)[:, i * 8:(i + 1) * 8], lhsT=hT[:, kc, i * 128:(i + 1) * 128], rhs=Wif[:, kc, :],
                                   start=(kc == 0), stop=(kc == 7))
            return ins
        P.op('pe', gmm, reads=['hT', 'Wif'], writes=['ps0'])
        P.op('dve', lambda e: e.tensor_tensor(out=pre, in0=bank(0)[:, 0:256].rearrange("p (i g) -> p i g", g=8),
                                              in1=bg.unsqueeze(1).broadcast_to([128, NT, 8]), op=ALU.add),
             reads=['ps0', 'bg'], writes=['pre'])
        P.op('act', lambda e: e.activation(out=tmpg, in_=pre[:, :, 4:8], func=AF.Exp, scale=-1.0), reads=['pre'], writes=['tmpg'])
        P.op('act', lambda e: e.activation(out=SPt, in_=tmpg, func=AF.Ln, bias=1.0), reads=['tmpg'], writes=['SPt'])
        P.op('dve', lambda e: e.tensor_scalar(out=v4(SPp_f)[:, :, 0, :], in0=v4(SPt_f)[:, :, 1, :], scalar1=fe, scalar2=None, op0=ALU.mult),
             reads=['SPt', 'pp'], writes=['SPp0'])
        P.op('dve', lambda e: e.tensor_scalar(out=v4(SPp_f)[:, :, 1, :], in0=v4(SPt_f)[:, :, 0, :], scalar1=fo, scalar2=None, op0=ALU.mult),
             reads=['SPt', 'pp'], writes=['SPp1'])
        P.op('dve', lambda e: e.tensor_tensor(out=SPs, in0=v4(SPt_f)[:, :, 0, :], in1=v4(SPt_f)[:, :, 1, :], op=ALU.add),
             reads=['SPt'], writes=['SPs'])

        def cmm(e):
            e.matmul(bank(1)[:, 0:128], lhsT=trif, rhs=SPt_f, start=True, stop=False)
            e.matmul(bank(1)[:, 0:128], lhsT=onesf, rhs=SPp_f, start=False, stop=True)
            return e.matmul(bank(2)[:, 0:64], lhsT=onesf, rhs=SPs_f, start=True, stop=True)
        P.op('pe', cmm, reads=['trif', 'onesf', 'SPt', 'SPp0', 'SPp1', 'SPs'], writes=['ps1', 'ps2'])
        psB = bank(1)[:, 0:128].rearrange("p (i h) -> p i h", h=4)
        psG = bank(2)[:, 0:64].rearrange("p (i h) -> p i h", h=4)
        P.op('dve', lambda e: e.tensor_tensor(out=tmpg, in0=pre[:, :, 0:4], in1=psB, op=ALU.add), reads=['pre', 'ps1', 'SPt'], writes=['tmpg'])
        P.op('act', lambda e: e.activation(out=As, in_=tmpg, func=AF.Exp), reads=['tmpg'], writes=['As'])
        P.op('act', lambda e: e.activation(out=Ct, in_=psB, func=AF.Exp, scale=-1.0, bias=math.log(128.0 ** -0.5)), reads=['ps1'], writes=['Ct'])
        P.op('act', lambda e: e.activation(out=Gd, in_=psG, func=AF.Exp, scale=-1.0), reads=['ps2'], writes=['Gd'])
        dbg_store('d_gates', As_f, 'As')
        dbg_store('d_ct', Ct_f, 'Ct')
        dbg_store('d_gd', Gd_f, 'Gd')

        qkn = [0]

        def qk_load(h):
            for wi, col0 in enumerate((C_MQ + h * 128, C_MK + h * 128)):
                wj = wi
                P.dma('pool', Wqk[wj], wslice(col0, 128), writes=['Wqk%d' % wj], dsem='wqk%d' % wj)

        def qk_block(h, wi, tb):
            wj = wi
            wk = 'Wqk%d' % wj
            cc = h if wi == 0 else 4 + h
            dst = (QTm2 if wi == 0 else KTm2)[h % 2]
            dk = ('QTm%d' if wi == 0 else 'KTm%d') % (h % 2)
            n_ = qkn[0]
            qkn[0] += 1
            s2 = n_ % 2

            def pm(e):
                for kc in range(8):
                    ins = e.matmul(bank(7), lhsT=Wqk[wj][:, kc, :], rhs=hT[:, kc, tb * 512:(tb + 1) * 512],
                                   start=(kc == 0), stop=(kc == 7))
                return ins
            P.op('pe', pm, reads=['hT', wk], writes=['ps7'])
            return (h, wi, tb, n_)

        def qk_blockA(ptok):
            h, wi, tb, n_ = ptok
            cc = h if wi == 0 else 4 + h
            dst = (QTm2 if wi == 0 else KTm2)[h % 2]
            dk = ('QTm%d' if wi == 0 else 'KTm%d') % (h % 2)
            s2 = n_ % 2
            uk = 'Uh%d' % tb
            ukp = ['Uh%d' % (tb - 1)] if tb > 0 else []
            if False:
                pass
            else:
                P.op('dve', lambda e: e.tensor_copy(out=Uh[:, 4 * tb:4 * tb + 4, 3:131], in_=bank(7).rearrange("p (a t) -> p a t", a=4)),
                     reads=['ps7'], writes=[uk])
            p0, p1 = 2 * tb, 2 * tb + 2
            hk = 'UhH%d' % tb
            if tb == 0:
                P.op('dve', lambda e: e.memset(Uh5[:, 0:1, 0, 0:3], 0.0), reads=[], writes=[hk])
                P.op('dve', lambda e: e.tensor_scalar(out=Uh5[:, 1:2, 0, 0:3], in0=Uh5[:, 0:1, 1, 128:131], scalar1=fo, scalar2=None,
                                                      op0=ALU.mult), reads=[uk, 'pp', hk], writes=[hk])
            else:
                P.op('dve', lambda e: e.tensor_scalar(out=Uh5[:, p0:p1, 0, 0:3], in0=Uh5[:, p0 - 1:p1 - 1, 1, 128:131], scalar1=fo, scalar2=None,
                                                      op0=ALU.mult), reads=[uk, 'pp'] + ukp, writes=[hk])
            P.op('dve', lambda e: e.scalar_tensor_tensor(out=Uh5[:, p0:p1, 0, 0:3], in0=Uh5[:, p0:p1, 1, 128:131], scalar=fe,
                                                         in1=Uh5[:, p0:p1, 0, 0:3], op0=ALU.mult, op1=ALU.add),
                 reads=[uk, hk], writes=[hk])
            P.op('dve', lambda e: e.tensor_scalar(out=Uh5[:, p0:p1, 1, 0:3], in0=Uh5[:, p0:p1, 0, 128:131], scalar1=fo, scalar2=None,
                                                  op0=ALU.mult), reads=[uk, hk], writes=[hk])
            if tb == 0:
                P.op('dve', lambda e: e.scalar_tensor_tensor(out=Uh5[:, 1:2, 1, 0:3], in0=Uh5[:, 0:1, 0, 128:131], scalar=fe,
                                                             in1=Uh5[:, 1:2, 1, 0:3], op0=ALU.mult, op1=ALU.add),
                     reads=[uk, hk], writes=[hk])
            else:
                P.op('dve', lambda e: e.scalar_tensor_tensor(out=Uh5[:, p0:p1, 1, 0:3], in0=Uh5[:, p0 - 1:p1 - 1, 0, 128:131], scalar=fe,
                                                             in1=Uh5[:, p0:p1, 1, 0:3], op0=ALU.mult, op1=ALU.add),
                     reads=[uk, hk] + ukp, writes=[hk])
            wcol = lambda j: pp[:, PP_CW + cc * 4 + j:PP_CW + cc * 4 + j + 1]
            bcol = pp[:, PP_CB + cc:PP_CB + cc + 1]
            ca = cacc2[s2]
            ck = 'cacc%d' % s2
            ub = Uh[:, 4 * tb:4 * tb + 4, :]
            P.op('dve', lambda e: e.tensor_scalar(out=ca, in0=ub[:, :, 3:131], scalar1=wcol(3), scalar2=bcol, op0=ALU.mult, op1=ALU.add),
                 reads=[uk, hk, 'pp'], writes=[ck])
            for j in (2, 1, 0):
                P.op('dve', lambda e, j=j: e.scalar_tensor_tensor(out=ca, in0=ub[:, :, j:j + 128], scalar=wcol(j), in1=ca,
                                                                  op0=ALU.mult, op1=ALU.add), reads=[uk, hk, ck], writes=[ck])
            return (s2, dst, dk, tb)

        def qk_blockB(tok):
            s2, dst, dk, tb = tok
            ca = cacc2[s2]
            ck = 'cacc%d' % s2
            st_ = stt[s2]
            sk_ = 'stt%d' % s2
            P.op('act', lambda e: e.activation(out=st_, in_=ca, func=AF.Exp, scale=-1.0), reads=[ck], writes=[sk_])
            P.op('act', lambda e: e.activation(out=st_, in_=st_, func=AF.Ln, bias=1.0), reads=[sk_], writes=[sk_])
            P.op('act', lambda e: e.activation(out=st_, in_=st_, func=AF.Exp, scale=-1.0), reads=[sk_], writes=[sk_])
            P.op('dve', lambda e: e.tensor_tensor(out=dst[:, tb * 512:(tb + 1) * 512].rearrange("p (a t) -> p a t", a=4), in0=ca, in1=st_,
                                                  op=ALU.mult), reads=[ck, sk_], writes=[dk])

        def stageA(h, p, sl):
            QTm, KTm = QTm2[h % 2], KTm2[h % 2]
            qk_, kk_ = 'QTm%d' % (h % 2), 'KTm%d' % (h % 2)
            wv, wo = 'Wv%d' % (h % 2), 'Wo%d' % (h % 2)

            def vm(e):
                for j in range(2):
                    i = 2 * p + j
                    for kc in range(8):
                        ins = e.matmul(bank(0)[:, j * 256:(j + 1) * 256], lhsT=hT[:, kc, i * 128:(i + 1) * 128], rhs=Wv[h % 2][:, kc, :],
                                       start=(kc == 0), stop=(kc == 7))
                return ins
            P.op('pe', vm, reads=['hT', wv], writes=['ps0'])
            for j in range(2):
                i = 2 * p + j
                P.op('act', lambda e, j=j, i=i: e.activation(out=Va[sl][:, j, 0:256], in_=bank(0)[:, j * 256:(j + 1) * 256], func=AF.Copy,
                                                             scale=As[:, i, h:h + 1]), reads=['ps0', 'As'], writes=['Va%d_%d' % (sl, j)])
            P.op('dve', lambda e: e.tensor_copy(out=Va[sl][:, :, 256:257], in_=As[:, 2 * p:2 * p + 2, h:h + 1]),
                 reads=['As'], writes=['Va%d_o' % sl])

            def kt(e):
                for j in range(2):
                    i = 2 * p + j
                    ins = e.transpose(out=bankb(1)[:, j * 128:(j + 1) * 128], in_=KTm[:, i * 128:(i + 1) * 128], identity=identb)
                return ins
            P.op('pe', kt, reads=[kk_, 'identb'], writes=['ps1'])
            P.op('act', lambda e: e.copy(out=Km[sl], in_=bankb(1)[:, 0:256].rearrange("p (j t) -> p j t", j=2)),
                 reads=['ps1'], writes=['Km%d' % sl])

            def mtm(e):
                for j in range(2):
                    ins = e.matmul(bank(2)[:, j * 128:(j + 1) * 128], lhsT=KTm[:, (2 * p + j) * 128:(2 * p + j + 1) * 128],
                                   rhs=QTm[:, 2 * p * 128:(2 * p + 1) * 128], start=True, stop=True)
                return ins
            P.op('pe', mtm, reads=[kk_, qk_], writes=['ps2'])
            P.op('dve', lambda e: e.tensor_tensor(out=MTb[sl][:, 0, :], in0=bank(2)[:, 0:128], in1=trib, op=ALU.mult),
                 reads=['ps2', 'trib'], writes=['MTb%d_0' % sl])
            P.op('dve', lambda e: e.tensor_scalar(out=MTb[sl][:, 1, :], in0=bank(2)[:, 128:256], scalar1=fe, scalar2=None, op0=ALU.mult),
                 reads=['ps2', 'pp'], writes=['MTb%d_1' % sl])

            def om(e):
                for kc in range(8):
                    ins = e.matmul(bank(3)[:, 0:256], lhsT=hT[:, kc, 2 * p * 128:(2 * p + 1) * 128], rhs=Wo[h % 2][:, kc, :],
                                   start=(kc == 0), stop=(kc == 7))
                return ins
            P.op('pe', om, reads=['hT', wo], writes=['ps3'])
            P.op('act', lambda e: e.activation(out=og[sl], in_=bank(3)[:, 0:256], func=AF.Exp, scale=-1.0), reads=['ps3'], writes=['og%d' % sl])
            P.op('act', lambda e: e.activation(out=og[sl], in_=og[sl], func=AF.Ln, bias=1.0), reads=['og%d' % sl], writes=['og%d' % sl])
            P.op('act', lambda e: e.activation(out=og[sl], in_=og[sl], func=AF.Exp, scale=-1.0), reads=['og%d' % sl], writes=['og%d' % sl])

        def stageB(h, p, sl):
            QTm, KTm = QTm2[h % 2], KTm2[h % 2]
            qk_, kk_ = 'QTm%d' % (h % 2), 'KTm%d' % (h % 2)
            vak = ['Va%d_0' % sl, 'Va%d_1' % sl, 'Va%d_o' % sl]

            def am(e):
                e.matmul(bank(4)[:, 0:257], lhsT=MTb[sl][:, 0, :], rhs=Va[sl][:, 0, 0:257], start=True, stop=False)
                ins = e.matmul(bank(4)[:, 0:257], lhsT=MTb[sl][:, 1, :], rhs=Va[sl][:, 1, 0:257], start=False, stop=(p == 0))
                if p > 0:
                    ins = e.matmul(bank(4)[:, 0:257], lhsT=QTm[:, 2 * p * 128:(2 * p + 1) * 128], rhs=Cbf[(p - 1) % 2][:, 0:257],
                                   start=False, stop=True)
                return ins
            P.op('pe', am, reads=['MTb%d_0' % sl, 'MTb%d_1' % sl, qk_, 'Cbf%d' % ((p - 1) % 2)] + vak, writes=['ps4'])
            if p < NPAIR - 1:
                def um(e):
                    e.matmul(bank(5)[:, 0:257], lhsT=Km[sl][:, 0, :], rhs=Va[sl][:, 0, 0:257], start=True, stop=False)
                    return e.matmul(bank(5)[:, 0:257], lhsT=Km[sl][:, 1, :], rhs=Va[sl][:, 1, 0:257], start=False, stop=True)
                P.op('pe', um, reads=['Km%d' % sl] + vak, writes=['ps5'])
                if p == 0:
                    P.op('dve', lambda e: e.tensor_copy(out=E[:, 0:257], in_=bank(5)[:, 0:257]), reads=['ps5'], writes=['E'])
                else:
                    P.op('dve', lambda e: e.scalar_tensor_tensor(out=E[:, 0:257], in0=E[:, 0:257], scalar=Gd[:, p - 1, h:h + 1],
                                                                 in1=bank(5)[:, 0:257], op0=ALU.mult, op1=ALU.add),
                         reads=['ps5', 'E', 'Gd'], writes=['E'])
                P.op('act', lambda e: e.activation(out=Cbf[p % 2][:, 0:257], in_=E[:, 0:257], func=AF.Copy, scale=Gd[:, p, h:h + 1]),
                     reads=['E', 'Gd'], writes=['Cbf%d' % (p % 2)])
            c = Ct[:, 2 * p, h:h + 1]
            s_ = scm[sl]
            sk = 'scm%d' % sl
            P.op('dve', lambda e: e.tensor_tensor(out=s_[:, 0:1], in0=bank(4)[:, 256:257], in1=c, op=ALU.mult),
                 reads=['ps4', 'Ct'], writes=[sk + 'a'])
            P.op('dve', lambda e: e.scalar_tensor_tensor(out=s_[:, 7:8], in0=s_[:, 0:1], scalar=-1.0, in1=s_[:, 0:1],
                                                         op0=ALU.mult, op1=ALU.max), reads=[sk + 'a'], writes=[sk + 'h'])
            P.op('dve', lambda e: e.tensor_scalar(out=s_[:, 1:2], in0=s_[:, 7:8], scalar1=1.0, scalar2=None, op0=ALU.max),
                 reads=[sk + 'h'], writes=[sk + 'b'])
            P.op('dve', lambda e: e.reciprocal(out=s_[:, 2:3], in_=s_[:, 1:2]), reads=[sk + 'b'], writes=[sk + 'c'])
            P.op('dve', lambda e: e.tensor_tensor(out=s_[:, 3:4], in0=s_[:, 2:3], in1=c, op=ALU.mult), reads=[sk + 'c', 'Ct'], writes=[sk + 'd'])
            P.op('act', lambda e: e.activation(out=hpre[sl], in_=bank(4)[:, 0:256], func=AF.Copy, scale=s_[:, 3:4]),
                 reads=['ps4', sk + 'd'], writes=['hpre%d' % sl])
            P.op('act', lambda e: e.activation(out=junkm, in_=hpre[sl], func=AF.Square, accum_out=s_[:, 4:5]),
                 reads=['hpre%d' % sl], writes=['junkm', sk + 'e'])
            P.op('act', lambda e: e.activation(out=s_[:, 5:6], in_=s_[:, 4:5], func=AF.Ln, scale=1.0 / 256, bias=EPS),
                 reads=[sk + 'e'], writes=[sk + 'f'])
            P.op('act', lambda e: e.activation(out=s_[:, 6:7], in_=s_[:, 5:6], func=AF.Exp, scale=-0.5), reads=[sk + 'f'], writes=[sk + 'g'])
            P.op('dve', lambda e: e.scalar_tensor_tensor(out=hn, in0=hpre[sl], scalar=s_[:, 6:7], in1=gbcB[:, h * 256:(h + 1) * 256],
                                                         op0=ALU.mult, op1=ALU.mult), reads=['hpre%d' % sl, sk + 'g', 'gbcB'], writes=['hn'])
            P.op('dve', lambda e: e.tensor_tensor(out=mtok[sl], in0=hn, in1=og[sl], op=ALU.mult), reads=['hn', 'og%d' % sl], writes=['mtok%d' % sl])


        def stageB2(h, p, sl):
            def tm(e):
                for j in range(2):
                    ins = e.transpose(out=bankb(6)[:, j * 128:(j + 1) * 128], in_=mtok[sl][:, j * 128:(j + 1) * 128], identity=identb)
                return ins
            P.op('pe', tm, reads=['mtok%d' % sl, 'identb'], writes=['ps6'])
            P.op('act', lambda e: e.copy(out=mT[:, 2 * h:2 * h + 2, p * 128:(p + 1) * 128],
                                        in_=bankb(6)[:, 0:256].rearrange("p (j t) -> p j t", j=2)), reads=['ps6'], writes=['mT'])

        qk_load(0)
        for wi in range(2):
            for tb in range(8):
                qk_blockB(qk_blockA(qk_block(0, wi, tb)))
        for h in range(4):
            P.dma('pool', Wv[h % 2], wslice(C_MV + h * 256, 256), writes=['Wv%d' % (h % 2)], dsem='wv%d' % (h % 2))
            P.dma('pool', Wo[h % 2], wslice(C_MO + h * 256, 256), writes=['Wo%d' % (h % 2)], dsem='wo%d' % (h % 2))
            blocks = []
            if h + 1 < 4:
                qk_load(h + 1)
                blocks = [(wi, tb) for wi in range(2) for tb in range(8)]
            tok = None
            stageA(h, 0, 0)
            for p in range(NPAIR):
                if p + 1 < NPAIR:
                    stageA(h, p + 1, (p + 1) % 2)
                ptok = qk_block(h + 1, *blocks[p]) if blocks else None
                stageB(h, p, p % 2)
                if p >= 1:
                    stageB2(h, p - 1, (p - 1) % 2)
                ntok = qk_blockA(ptok) if ptok is not None else None
                if tok is not None:
                    qk_blockB(tok)
                tok = ntok
            stageB2(h, NPAIR - 1, (NPAIR - 1) % 2)
            if tok is not None:
                qk_blockB(tok)
        if 'd_mT' in dbg_out:
            P.op('dve', lambda e: e.tensor_copy(out=gbcA.rearrange("p (k t) -> p k t", k=8), in_=mT[:, :, 1920:2048]), reads=['mT'], writes=['gbcA'])
            dbg_store('d_mT', gbcA, 'gbcA')
        P.barrier()
        if phases <= 1:
            P.finish()
            return nc

        TA = Arena(big, T0, TOT)
        KTa = TA.take([128, S], BF16)
        KTb = TA.take([128, S], BF16)
        Vg = TA.take([128, NT, 130], BF16)
        QTa = TA.take([128, S // 2], BF16)
        QTb = TA.take([128, S // 2], BF16)
        Pb = [[TA.take([128, 512], BF16) for _ in range(2)] for _ in range(2)]
        maskb = TA.take([128, 8, 512], BF16)
        Wd = [[TA.take([128, 8, 128], BF16) for _ in range(3)] for _ in range(2)]
        otmp = [TA.take([128, 128], F32) for _ in range(4)]
        ofin = [TA.take([128, 128], F32) for _ in range(4)]
        abf = [TA.take([128, 128], BF16) for _ in range(4)]
        junka = TA.take([128, 128], BF16)
        sca = [TA.take([128, 16], F32) for _ in range(4)]

        P.dma('sp', gbcA, vb[:, VB_DAG:VB_DAG + 1024].partition_broadcast(128), writes=['gbcA'], dsem='gA')
        P.op('dve', lambda e: e.tensor_scalar(out=gbcA, in0=gbcA, scalar1=1.0 - LAM_INIT, scalar2=None, op0=ALU.mult),
             reads=['gbcA'], writes=['gbcA'])
        P.op('dve', lambda e: e.memset(Vg[:, :, 128:129], 1.0), reads=[], writes=['Vg1'])
        P.op('dve', lambda e: e.memset(KTa[64:128, :], 0.0), reads=[], writes=['KTaug_a'])
        P.op('dve', lambda e: e.memset(KTb[0:64, :], 0.0), reads=[], writes=['KTaug_b'])
        P.op('dve', lambda e: e.memset(QTa[64:128, :], 0.0), reads=[], writes=['QTaug_a'])
        P.op('dve', lambda e: e.memset(QTb[0:64, :], 0.0), reads=[], writes=['QTaug_b'])
        P.dma('pool', KTa[64:68, :], kaugd, writes=['KTaug_a'], dsem='kaug_a')
        P.dma('pool', KTb[0:4, :], kaugd, writes=['KTaug_b'], dsem='kaug_b')

        def load_wd(h):
            sl = h % 2
            for j, c0 in enumerate((C_Q, C_K, C_V)):
                P.dma('pool', Wd[sl][j], wslice(c0 + h * 128, 128), writes=['Wd%d_%d' % (sl, j)], dsem='wd%d_%d' % (sl, j))

        cpy_i = [0]
        PB = [7, 0, 1, 2, 3] if FLAGS['pb'] else [7, 7, 7, 7, 7]
        dve_only = [FLAGS['dveonly']]

        def evac(out_ap, in_ap, rk, wk):
            cpy_i[0] += 1
            if cpy_i[0] % 2 or dve_only[0]:
                P.op('dve', lambda e: e.tensor_copy(out=out_ap, in_=in_ap), reads=rk, writes=wk)
            else:
                P.op('act', lambda e: e.copy(out=out_ap, in_=in_ap), reads=rk, writes=wk)

        def proj_k(h):
            sl = h % 2
            for tb in range(8):
                bk = PB[tb % 5]

                def pm(e, tb=tb, bk=bk):
                    for kc in range(8):
                        ins = e.matmul(bank(bk), lhsT=Wd[sl][1][:, kc, :], rhs=hT[:, kc, tb * 512:(tb + 1) * 512],
                                       start=(kc == 0), stop=(kc == 7))
                    return ins
                P.op('pe', pm, reads=['hT', 'Wd%d_1' % sl], writes=['ps%d' % bk])
                evac(KTa[0:64, tb * 512:(tb + 1) * 512], bank(bk)[0:64, :], ['ps%d' % bk], ['KTa%d' % tb])
                evac(KTb[64:128, tb * 512:(tb + 1) * 512], bank(bk)[64:128, :], ['ps%d' % bk], ['KTb%d' % tb])

        def proj_qv(h):
            sl = h % 2
            P.dma('pool', QTa[64:68, :], qaugd[h], writes=['QTaug_a'], dsem='qaug_a')
            P.dma('pool', QTb[0:4, :], qaugd[h], writes=['QTaug_b'], dsem='qaug_b')
            for m in range(4):
                bk = PB[(m + 3) % 5]

                def pm(e, m=m, bk=bk):
                    for kc in range(8):
                        ins = e.matmul(bank(bk), lhsT=Wd[sl][0][:, kc, :], rhs=hT5[:, kc, 4 * m:4 * m + 4, 0, :],
                                       start=(kc == 0), stop=(kc == 7))
                    return ins
                P.op('pe', pm, reads=['hT', 'Wd%d_0' % sl], writes=['ps%d' % bk])
                evac(QTa[0:64, m * 512:(m + 1) * 512], bank(bk)[0:64, :], ['ps%d' % bk], ['QTa%d' % m])
                evac(QTb[64:128, m * 512:(m + 1) * 512], bank(bk)[64:128, :], ['ps%d' % bk], ['QTb%d' % m])
            for i4 in range(8):
                bk = PB[(i4 + 2) % 5]

                def vm(e, i4=i4, bk=bk):
                    for j in range(4):
                        i = 4 * i4 + j
                        for kc in range(8):
                            ins = e.matmul(bank(bk)[:, j * 128:(j + 1) * 128], lhsT=hT[:, kc, i * 128:(i + 1) * 128],
                                           rhs=Wd[sl][2][:, kc, :], start=(kc == 0), stop=(kc == 7))
                    return ins
                P.op('pe', vm, reads=['hT', 'Wd%d_2' % sl], writes=['ps%d' % bk])
                evac(Vg[:, 4 * i4:4 * i4 + 4, 0:128], bank(bk).rearrange("p (j d) -> p j d", j=4), ['ps%d' % bk], ['Vg%d' % i4])

        def smm(h, m, i, a):
            general = i >= 8 * m
            c0 = ((i - 8 * m) // 2) * 128 if (general and FLAGS['trimmm']) else 0
            KTx = KTb if a else KTa
            QTx = QTb if a else QTa
            ab = 'b' if a else 'a'
            bk = (2 * (i % 2) + a) if FLAGS['dbl'] else a

            def f(e):
                ins = e.matmul(bank(bk)[:, c0:512], lhsT=KTx[:, i * 128:(i + 1) * 128], rhs=QTx[:, m * 512 + c0:(m + 1) * 512],
                               start=True, stop=(not general))
                if general:
                    if FLAGS['mask128']:
                        u0 = (i - 8 * m) // 2
                        ins = e.matmul(bank(bk)[:, u0 * 128:(u0 + 1) * 128], lhsT=identb, rhs=maskb[:, i - 8 * m, u0 * 128:(u0 + 1) * 128],
                                       start=False, stop=True)
                    else:
                        ins = e.matmul(bank(bk)[:, c0:512], lhsT=identb, rhs=maskb[:, i - 8 * m, c0:512], start=False, stop=True)
                return ins
            P.op('pe', f, reads=['KT%s%d' % (ab, i // 4), 'QT%s%d' % (ab, m), 'KTaug_' + ab, 'QTaug_' + ab, 'maskb', 'identb'],
                 writes=['ps%d' % bk])

        def expo(h, m, i, a, bs):
            general = i >= 8 * m
            c0 = ((i - 8 * m) // 2) * 128 if (general and FLAGS['trim']) else 0
            sl = i % 2
            bk = (2 * (i % 2) + a) if FLAGS['dbl'] else a
            P.op('act', lambda e: e.activation(out=Pb[a][sl][:, c0:512], in_=bank(bk)[:, c0:512], func=AF.Exp, scale=0.125),
                 reads=['ps%d' % bk], writes=['Pb%d_%d' % (a, sl)])

        def oreg(a, u):
            if u < 3:
                return bank(4 + a)[:, u * 129:(u + 1) * 129], 'ps%d' % (4 + a)
            return bank(6)[:, a * 129:(a + 1) * 129], 'ps6'

        def omm(h, m, i, a):
            general = i >= 8 * m
            r = i - 8 * m
            sl = i % 2
            us = [u for u in range(4) if not (general and (r // 2 > u))]

            def f(e):
                for u in us:
                    o_, _ = oreg(a, u)
                    last = (i == 8 * m + 2 * u + 1)
                    ins = e.matmul(o_, lhsT=Pb[a][sl][:, u * 128:(u + 1) * 128], rhs=Vg[:, i, 0:129],
                                   start=(i == 0 and (u == 0 or (u == 3 and a == 0))), stop=last, skip_group_check=True)
                return ins
            P.op('pe', f, reads=['Pb%d_%d' % (a, sl), 'Vg%d' % (i // 4), 'Vg1'], writes=['ps%d' % (4 + a), 'ps6'])

        def post1(h, m):
            regs = [(oreg(0, u), oreg(1, u)) for u in range(4)]
            for u in range(4):
                (O0u, k0), (O1u, k1) = regs[u]
                s_ = sca[u]
                sk = 'sca%d' % u
                P.op('dve', lambda e, s_=s_, O0u=O0u: e.reciprocal(out=s_[:, 0:1], in_=O0u[:, 128:129]), reads=[k0], writes=[sk + 'a'])
                P.op('dve', lambda e, s_=s_, O1u=O1u: e.reciprocal(out=s_[:, 1:2], in_=O1u[:, 128:129]), reads=[k1], writes=[sk + 'b'])
            for u in range(4):
                s_ = sca[u]
                sk = 'sca%d' % u
                P.op('dve', lambda e, s_=s_: e.tensor_tensor(out=s_[:, 2:3], in0=s_[:, 1:2], in1=small[:, 2:3], op=ALU.mult),
                     reads=[sk + 'b', 'neglam'], writes=[sk + 'c'])
            for u in range(4):
                (O0u, k0), (O1u, k1) = regs[u]
                s_ = sca[u]
                sk = 'sca%d' % u
                P.op('dve', lambda e, s_=s_, O1u=O1u, u=u: e.tensor_scalar(out=otmp[u], in0=O1u[:, 0:128], scalar1=s_[:, 2:3], scalar2=None,
                                                                           op0=ALU.mult), reads=[k1, sk + 'c'], writes=['otmp%d' % u])
            for u in range(4):
                (O0u, k0), (O1u, k1) = regs[u]
                s_ = sca[u]
                sk = 'sca%d' % u
                P.op('dve', lambda e, s_=s_, O0u=O0u, u=u: e.scalar_tensor_tensor(out=ofin[u], in0=O0u[:, 0:128], scalar=s_[:, 0:1], in1=otmp[u],
                                                                                  op0=ALU.mult, op1=ALU.add),
                     reads=[k0, sk + 'a', 'otmp%d' % u], writes=['ofin%d' % u])

        def post1b(h, m):
            for u in range(4):
                s_ = sca[u]
                sk = 'sca%d' % u
                P.op('act', lambda e, s_=s_, u=u: e.activation(out=junka, in_=ofin[u], func=AF.Square, accum_out=s_[:, 3:4]),
                     reads=['ofin%d' % u], writes=['junka', sk + 'd'])
                P.op('act', lambda e, s_=s_: e.activation(out=s_[:, 4:5], in_=s_[:, 3:4], func=AF.Ln, scale=1.0 / 128, bias=EPS),
                     reads=[sk + 'd'], writes=[sk + 'e'])
                P.op('act', lambda e, s_=s_: e.activation(out=s_[:, 5:6], in_=s_[:, 4:5], func=AF.Exp, scale=-0.5),
                     reads=[sk + 'e'], writes=[sk + 'f'])
                P.op('dve', lambda e, s_=s_, u=u: e.scalar_tensor_tensor(out=abf[u], in0=ofin[u], scalar=s_[:, 5:6],
                                                                         in1=gbcA[:, h * 128:(h + 1) * 128], op0=ALU.mult, op1=ALU.mult),
                     reads=['ofin%d' % u, sk + 'f', 'gbcA'], writes=['abf%d' % u])

        def post2(h, m):
            def tm(e):
                for u in range(4):
                    ins = e.transpose(out=bankb(7)[:, u * 128:(u + 1) * 128], in_=abf[u], identity=identb)
                return ins
            P.op('pe', tm, reads=['abf0', 'abf1', 'abf2', 'abf3', 'identb'], writes=['ps7'])
            evac(aT[:, h, m * 512:(m + 1) * 512], bankb(7)[:, 0:512], ['ps7'], ['aT'])

        pending = []
        pend_b = []
        load_wd(0)
        P.dma('pool', maskb, mkd.rearrange("p (r q) -> p r q", r=8), writes=['maskb'], dsem='maskb')
        bsn = [0]
        for h in range(8):
            if h + 1 < 8:
                load_wd(h + 1)
            proj_k(h)
            while pend_b:
                post1b(*pend_b.pop(0))
            while pending:
                post2(*pending.pop(0))
            proj_qv(h)
            for m in range(4):
                nt = 8 * m + 8
                bs = 0
                smm(h, m, 0, 0)
                smm(h, m, 0, 1)
                for i in range(nt):
                    expo(h, m, i, 0, bs)
                    expo(h, m, i, 1, bs)
                    if i + 1 < nt:
                        smm(h, m, i + 1, 0)
                        smm(h, m, i + 1, 1)
                    omm(h, m, i, 0)
                    omm(h, m, i, 1)
                    if i == 1:
                        while pend_b:
                            post1b(*pend_b.pop(0))
                    if i == 3:
                        while pending:
                            post2(*pending.pop(0))
                post1(h, m)
                pend_b.append((h, m))
                pending.append((h, m))
        while pend_b:
            post1b(*pend_b.pop(0))
        while pending:
            post2(*pending.pop(0))
        for nm, lo, n in (('d_aT', 0, 128), ('d_aT2', 1920, 128)):
            if nm in dbg_out:
                P.op('dve', lambda e, lo=lo, n=n: e.tensor_copy(out=gbcB.rearrange("p (k t) -> p k t", k=8), in_=aT[:, :, lo:lo + n]),
                     reads=['aT'], writes=['gbcB'])
                dbg_store(nm, gbcB, 'gbcB')
        dve_only[0] = False
        P.barrier()
        if phases <= 2:
            P.finish()
            return nc

        TA = Arena(big, T0, TOT)
        mgT = TA.take([128, 8, S // 2], BF16)
        Wm = [[TA.take([128, 8, 128], BF16) for _ in range(4)] for _ in range(2)]
        sig = [[TA.take([128, 512], F32) for _ in range(2)] for _ in range(2)]
        t1 = [TA.take([128, 512], F32) for _ in range(2)]
        t2 = [TA.take([128, 512], F32) for _ in range(2)]

        def wsq(w, c0, n):
            return w[:, c0:c0 + n].rearrange("(kc p) f -> p kc f", p=128)

        def load_wm(dc):
            sl = dc % 2
            srcs = (wslice(C_MG + dc * 128, 128), wslice(C_MG + 1024 + dc * 128, 128), wsq(w_a, dc * 128, 128), wsq(w_m, dc * 128, 128))
            for j, src in enumerate(srcs):
                P.dma('pool', Wm[sl][j], src, writes=['Wm%d_%d' % (sl, j)], dsem='wm%d_%d' % (sl, j))

        load_wm(0)
        it = 0
        for dc in range(8):
            if dc + 1 < 8:
                load_wm(dc + 1)
            sl = dc % 2
            for m in range(4):
                b0 = 4 * (it % 2)
                s2 = it % 2
                it += 1
                rhss = (lambda kc, m=m: hT5[:, kc, 4 * m:4 * m + 4, 0, :], lambda kc, m=m: hT5[:, kc, 4 * m:4 * m + 4, 0, :],
                        lambda kc, m=m: aT[:, kc, m * 512:(m + 1) * 512], lambda kc, m=m: mT[:, kc, m * 512:(m + 1) * 512])
                rkeys = ('hT', 'hT', 'aT', 'mT')
                for j in range(4):
                    def pm(e, j=j, b0=b0):
                        for kc in range(8):
                            ins = e.matmul(bank(b0 + j), lhsT=Wm[sl][j][:, kc, :], rhs=rhss[j](kc), start=(kc == 0), stop=(kc == 7))
                        return ins
                    P.op('pe', pm, reads=[rkeys[j], 'Wm%d_%d' % (sl, j)], writes=['ps%d' % (b0 + j)])
                for j in range(2):
                    bcol = pp[:, PP_BM + 8 * j + dc:PP_BM + 8 * j + dc + 1]
                    P.op('act', lambda e, j=j, b0=b0, bcol=bcol, s2=s2: e.activation(out=sig[s2][j], in_=bank(b0 + j), func=AF.Sigmoid, bias=bcol),
                         reads=['ps%d' % (b0 + j), 'pp'], writes=['sig%d_%d' % (s2, j)])
                P.op('dve', lambda e, b0=b0, s2=s2: e.tensor_tensor(out=t1[s2], in0=bank(b0 + 2), in1=sig[s2][0], op=ALU.mult),
                     reads=['ps%d' % (b0 + 2), 'sig%d_0' % s2], writes=['t1_%d' % s2])
                P.op('dve', lambda e, b0=b0, s2=s2: e.tensor_tensor(out=t2[s2], in0=bank(b0 + 3), in1=sig[s2][1], op=ALU.mult),
                     reads=['ps%d' % (b0 + 3), 'sig%d_1' % s2], writes=['t2_%d' % s2])
                P.op('dve', lambda e, s2=s2, dc=dc, m=m: e.tensor_tensor(out=mgT[:, dc, m * 512:(m + 1) * 512], in0=t1[s2], in1=t2[s2], op=ALU.add),
                     reads=['t1_%d' % s2, 't2_%d' % s2], writes=['mgT'])
        P.barrier()

        TA = Arena(big, T0 + 32768, TOT)
        Wout = TA.take([128, 8, 1024], BF16)
        xin2 = [TA.take([128, 1024], F32) for _ in range(2)]
        hmb = [TA.take([128, 1024], BF16) for _ in range(2)]
        junk4 = TA.take([128, 1024], BF16)
        scw = TA.take([128, 4 * NPAIR], F32)
        x1 = Arena(big, R1, R2).take([128, NPAIR, 1024], F32)
        hmT = Arena(big, R2, R3).take([128, 8, S // 2], BF16)
        uT = Arena(big, R3, T0).take([128, 8, S // 2], BF16)
        for half in range(2):
            P.dma('pool', Wout[:, :, half * 512:(half + 1) * 512], wsq(w_o, half * 512, 512), writes=['Wout%d' % half], dsem='wout%d' % half)
        P.dma('sp', gbcA, vb[:, VB_MLP:VB_MLP + 1024].partition_broadcast(128), writes=['gbcA'], dsem='gA')
        P.dma('sp', gbcB, vb[:, VB_FIN:VB_FIN + 1024].partition_broadcast(128), writes=['gbcB'], dsem='gB')
        xown = xl.rearrange("(q e p) d -> q e p d", e=2, p=128)
        pend4 = []
        for t in range(NPAIR):
            s2 = t % 2
            P.dma('sp', xin2[s2], xown[t, 0], writes=['xin2_%d' % s2], dsem='xin2_%d' % s2)
            for half in range(2):
                bk = 2 * s2 + half

                def pm(e, t=t, half=half, bk=bk):
                    for kc in range(8):
                        ins = e.matmul(bank(bk), lhsT=mgT[:, kc, t * 128:(t + 1) * 128], rhs=Wout[:, kc, half * 512:(half + 1) * 512],
                                       start=(kc == 0), stop=(kc == 7))
                    return ins
                P.op('pe', pm, reads=['mgT', 'Wout%d' % half], writes=['ps%d' % bk])
                P.op('dve', lambda e, t=t, half=half, bk=bk, s2=s2: e.tensor_tensor(
                    out=x1[:, t, half * 512:(half + 1) * 512], in0=bank(bk), in1=xin2[s2][:, half * 512:(half + 1) * 512], op=ALU.add),
                    reads=['ps%d' % bk, 'xin2_%d' % s2], writes=['x1_%d_%d' % (t, half)])
            xk = ['x1_%d_0' % t, 'x1_%d_1' % t]
            P.op('act', lambda e, t=t: e.activation(out=junk4, in_=x1[:, t, :], func=AF.Square, accum_out=scw[:, 4 * t:4 * t + 1]),
                 reads=xk, writes=['junk4', 'scw%d_a' % t])
            P.op('act', lambda e, t=t: e.activation(out=scw[:, 4 * t + 1:4 * t + 2], in_=scw[:, 4 * t:4 * t + 1], func=AF.Ln, scale=1.0 / D, bias=EPS),
                 reads=['scw%d_a' % t], writes=['scw%d_b' % t])
            P.op('act', lambda e, t=t: e.activation(out=scw[:, 4 * t + 2:4 * t + 3], in_=scw[:, 4 * t + 1:4 * t + 2], func=AF.Exp, scale=-0.5),
                 reads=['scw%d_b' % t], writes=['scw%d_c' % t])
            P.op('dve', lambda e, t=t, s2=s2: e.scalar_tensor_tensor(out=hmb[s2], in0=x1[:, t, :], scalar=scw[:, 4 * t + 2:4 * t + 3], in1=gbcA,
                                                                     op0=ALU.mult, op1=ALU.mult),
                 reads=xk + ['scw%d_c' % t, 'gbcA'], writes=['hmb%d' % s2])

            def tr4(t=t, s2=s2):
                def f(e):
                    for kc in range(8):
                        ins = e.transpose(out=bankb(4 + s2)[:, kc * 128:(kc + 1) * 128], in_=hmb[s2][:, kc * 128:(kc + 1) * 128], identity=identb)
                    return ins
                P.op('pe', f, reads=['hmb%d' % s2, 'identb'], writes=['ps%d' % (4 + s2)])
                evac(hmT[:, :, t * 128:(t + 1) * 128], bankb(4 + s2).rearrange("p (k t) -> p k t", k=8), ['ps%d' % (4 + s2)], ['hmT'])
            if pend4:
                pend4.pop(0)()
            pend4.append(tr4)
        while pend4:
            pend4.pop(0)()
        P.barrier()

        TA = Arena(big, T0, TOT)
        Wf = [TA.take([128, 8, 1024], BF16) for _ in range(3)]
        rl = [TA.take([128, 512], F32) for _ in range(2)]
        outt = [TA.take([128, 1024], F32) for _ in range(2)]
        junk5 = TA.take([128, 1024], BF16)
        scf = TA.take([128, 4 * NPAIR], F32)

        def load_wf(n):
            g = n // 2
            if n % 2 == 0:
                src = w_1[:, g * 1024:(g + 1) * 1024].rearrange("(kc p) f -> p kc f", p=128)
            else:
                src = w_2[g * 1024:(g + 1) * 1024, :].rearrange("(fc p) d -> p fc d", p=128)
            if n == 0:
                for fcl in range(8):
                    P.dma('pool', Wf[0][:, :, fcl * 128:(fcl + 1) * 128], src[:, :, fcl * 128:(fcl + 1) * 128],
                          writes=['Wf0c%d' % fcl], dsem='wf0c%d' % fcl)
                return
            extra = ['Wf0c%d' % f_ for f_ in range(8)] if n % 3 == 0 else []
            P.dma('pool', Wf[n % 3], src, writes=['Wf%d' % (n % 3)] + extra, dsem='wf%d' % (n % 3))

        for n in range(3):
            load_wf(n)
        out3 = out.rearrange("(t p) d -> t p d", p=128)
        j4 = 0
        for g in range(4):
            W1 = Wf[(2 * g) % 3]
            W2 = Wf[(2 * g + 1) % 3]
            k1, k2 = 'Wf%d' % ((2 * g) % 3), 'Wf%d' % ((2 * g + 1) % 3)
            for fcl in range(8):
                for m in range(4):
                    bk = j4 % 4
                    s2 = j4 % 2
                    j4 += 1

                    def pm(e, fcl=fcl, m=m, bk=bk, W1=W1):
                        for kc in range(8):
                            ins = e.matmul(bank(bk), lhsT=W1[:, kc, fcl * 128:(fcl + 1) * 128], rhs=hmT[:, kc, m * 512:(m + 1) * 512],
                                           start=(kc == 0), stop=(kc == 7))
                        return ins
                    P.op('pe', pm, reads=['hmT', ('Wf0c%d' % fcl) if g == 0 else k1], writes=['ps%d' % bk])
                    P.op('act', lambda e, bk=bk, s2=s2: e.activation(out=rl[s2], in_=bank(bk), func=AF.Relu), reads=['ps%d' % bk], writes=['rl%d' % s2])
                    P.op('dve', lambda e, s2=s2, fcl=fcl, m=m: e.tensor_tensor(out=uT[:, fcl, m * 512:(m + 1) * 512], in0=rl[s2], in1=rl[s2], op=ALU.mult),
                         reads=['rl%d' % s2], writes=['uT%d' % fcl])
            if 2 * g + 3 < 8:
                load_wf(2 * g + 3)
            for t in range(NPAIR):
                for half in range(2):
                    bk = 4 + (2 * t + half) % 4

                    def pm2(e, t=t, half=half, bk=bk, W2=W2):
                        for fcl in range(8):
                            ins = e.matmul(bank(bk), lhsT=uT[:, fcl, t * 128:(t + 1) * 128], rhs=W2[:, fcl, half * 512:(half + 1) * 512],
                                           start=(fcl == 0), stop=(fcl == 7))
                        return ins
                    P.op('pe', pm2, reads=['uT%d' % f_ for f_ in range(8)] + [k2], writes=['ps%d' % bk])
                    P.op('dve', lambda e, t=t, half=half, bk=bk: e.tensor_tensor(
                        out=x1[:, t, half * 512:(half + 1) * 512], in0=bank(bk), in1=x1[:, t, half * 512:(half + 1) * 512], op=ALU.add),
                        reads=['ps%d' % bk, 'x1_%d_%d' % (t, half)], writes=['x1_%d_%d' % (t, half)])
                if g == 3:
                    s2 = t % 2
                    xk = ['x1_%d_0' % t, 'x1_%d_1' % t]
                    P.op('act', lambda e, t=t: e.activation(out=junk5, in_=x1[:, t, :], func=AF.Square, accum_out=scf[:, 4 * t:4 * t + 1]),
                         reads=xk, writes=['junk5', 'scf%d_a' % t])
                    P.op('act', lambda e, t=t: e.activation(out=scf[:, 4 * t + 1:4 * t + 2], in_=scf[:, 4 * t:4 * t + 1], func=AF.Ln,
                                                            scale=1.0 / D, bias=EPS), reads=['scf%d_a' % t], writes=['scf%d_b' % t])
                    P.op('act', lambda e, t=t: e.activation(out=scf[:, 4 * t + 2:4 * t + 3], in_=scf[:, 4 * t + 1:4 * t + 2], func=AF.Exp, scale=-0.5),
                         reads=['scf%d_b' % t], writes=['scf%d_c' % t])
                    P.op('dve', lambda e, t=t, s2=s2: e.scalar_tensor_tensor(out=outt[s2], in0=x1[:, t, :], scalar=scf[:, 4 * t + 2:4 * t + 3], in1=gbcB,
                                                                             op0=ALU.mult, op1=ALU.mult),
                         reads=xk + ['scf%d_c' % t, 'gbcB'], writes=['outt%d' % s2])
                    P.dma('sp', out3[t], outt[s2], reads=['outt%d' % s2], dsem='out%d' % s2)
            if 2 * g + 4 < 8:
                load_wf(2 * g + 4)
        P.finish()
    return nc


def _tile_perm(c):
    perm = np.zeros(NT, np.int64)
    for p in range(NPAIR):
        perm[2 * p] = 2 * p + c
        perm[2 * p + 1] = 2 * p + (1 - c)
    return perm


def _make_pp(c, conv_w, conv_b, b_merge):
    pp = np.zeros((128, PP_N), np.float32)
    cw = conv_w.reshape(4, 8, 128)
    for ch in range(8):
        for j in range(4):
            pp[:, PP_CW + ch * 4 + j] = cw[j, ch]
    pp[:, PP_CB:PP_CB + 8] = conv_b.reshape(8, 128).T
    pp[:, PP_BM:PP_BM + 16] = b_merge.reshape(16, 128).T
    perm = _tile_perm(c)
    for i in range(NT):
        pp[:, PP_KPOS + i] = perm[i] * 128 + np.arange(128)
    for p in range(NPAIR):
        pp[:, PP_Q128 + p] = (2 * p + c) * 128 + 64
    for v in range(8):
        pp[:, PP_Q256 + v] = (4 * v + c) * 128 + 192
    for m in range(4):
        pp[:, PP_Q512 + m] = (8 * m + c) * 128 + 448
    pp[:, PP_FE] = float(c)
    pp[:, PP_FO] = float(1 - c)
    return pp


def _make_mask(c):
    mk = np.zeros((8, 128, 512), np.float32)
    kk = np.arange(128)[:, None]
    qq = np.arange(128)[None, :]
    for r in range(8):
        pk, par = r // 2, r % 2
        for u in range(4):
            blk = mk[r, :, u * 128:(u + 1) * 128]
            if pk < u:
                pass
            elif pk > u:
                blk[:] = NEG
            else:
                if par == 0:
                    blk[:] = np.where(kk <= qq, 0.0, NEG)
                else:
                    blk[:] = 0.0 if c == 1 else NEG
    return mk.transpose(1, 0, 2).reshape(128, 8 * 512).copy()


def _make_aug(c):
    perm = _tile_perm(c)
    kpos = (perm[:, None] * 128 + np.arange(128)[None, :]).reshape(-1)
    kaug = np.stack([kpos // 64, kpos % 64, np.ones_like(kpos), np.ones_like(kpos)]).astype(np.float32)
    qpos = np.array([(2 * p + c) * 128 + j for p in range(NPAIR) for j in range(128)])
    qaug = np.zeros((8, 4, S // 2), np.float32)
    for h in range(8):
        sl = 2.0 ** -(h + 1)
        qaug[h, 0] = 512.0 * sl
        qaug[h, 1] = 8.0 * sl
        qaug[h, 2] = -512.0 * sl * (qpos // 64)
        qaug[h, 3] = -8.0 * sl * (qpos % 64)
    return kaug, qaug


def _prep_inputs(x, norm_mix_g, w_in, b_gates, conv_w, conv_b, lam, da_norm_g, ml_norm_g,
                 b_merge, w_branch_a, w_branch_m, w_out, norm_mlp_g, w_ff1, w_ff2, norm_final_g):
    f = lambda a: np.ascontiguousarray(np.asarray(a, dtype=np.float32))
    x = f(x)
    vbrow = np.concatenate([f(norm_mix_g).reshape(-1), f(da_norm_g).reshape(-1), f(ml_norm_g).reshape(-1),
                            f(norm_mlp_g).reshape(-1), f(norm_final_g).reshape(-1), f(b_gates).reshape(-1),
                            f(lam).reshape(-1)]).reshape(1, VB_N)
    cst = np.concatenate([np.eye(128, dtype=np.float32), np.triu(np.ones((128, 128), np.float32)),
                          np.ones((128, 128), np.float32)], axis=1)
    shared = dict(w_in=f(w_in)[0], w_a=f(w_branch_a)[0], w_m=f(w_branch_m)[0], w_o=f(w_out)[0],
                  w_1=f(w_ff1)[0], w_2=f(w_ff2)[0], vb=vbrow, cst=cst)
    in_maps = []
    for core in range(8):
        b, c = core // 2, core % 2
        perm = _tile_perm(c)
        xl = x[b].reshape(NT, 128, D)[perm].reshape(S, D)
        m = dict(shared)
        m['xl'] = np.ascontiguousarray(xl)
        m['pp'] = _make_pp(c, f(conv_w)[0], f(conv_b)[0], f(b_merge)[0])
        m['mk'] = _make_mask(c)
        m['kaug'], m['qaug'] = _make_aug(c)
        in_maps.append(m)
    return in_maps


def kernel(**inputs):
    in_maps = _prep_inputs(**inputs)
    nc = build()
    res = run_bass_kernel_spmd(nc, in_maps, core_ids=list(range(8)))
    outp = np.zeros((4, S, D), np.float32)
    for core in range(8):
        b, c = core // 2, core % 2
        o = np.asarray(res.results[core]["out"]).reshape(NPAIR, 128, D)
        ov = outp[b].reshape(NT, 128, D)
        for p in range(NPAIR):
            ov[2 * p + c] = o[p]
    return outp
```

```python
import numpy as np
import ml_dtypes
import concourse.bass as bass
import concourse.mybir as mybir
from concourse.bass_utils import run_bass_kernel_spmd

F32 = mybir.dt.float32
BF16 = mybir.dt.bfloat16
AF = mybir.ActivationFunctionType
ALU = mybir.AluOpType

S = 4096
D = 1024
NT = 32
NPAIR = 16
DIN = 8200
EPS = 1e-6
LAM_INIT = 0.2
NEG = -30000.0
DBG = {}
FLAGS = {'mask128': True, 'b2late': True, 'qkint': True, 'trimmm': False, 'trim': True, 'dveonly': True, 'dbl': True, 'pb': True}


class Prog:
    def __init__(self, nc):
        self.nc = nc
        self.engs = {'pe': nc.tensor, 'dve': nc.vector, 'act': nc.scalar,
                     'pool': nc.gpsimd, 'sp': nc.sync}
        self.sem = {k: nc.alloc_semaphore(name='s_' + k) for k in self.engs}
        self.bar = nc.alloc_semaphore(name='s_bar')
        self.nbar = 0
        self.cnt = {k: 0 for k in self.engs}
        self.dsem = {}
        self.dcnt = {}
        self.waited = {}
        self.last_w = {}
        self.readers = {}

    def _semh(self, key):
        return self.sem[key] if key in self.sem else self.dsem[key]

    def _wait(self, eng, deps):
        best = {}
        for (k, v) in deps:
            if k == 'pe' and eng == 'pe':
                continue
            if v > best.get(k, 0):
                best[k] = v
        for k, v in best.items():
            if self.waited.get((eng, k), 0) >= v:
                continue
            self.engs[eng].wait_ge(self._semh(k), v)
            self.waited[(eng, k)] = v

    def _deps(self, reads, writes):
        deps = set()
        for b in reads:
            if b in self.last_w:
                deps.add(self.last_w[b])
        for b in writes:
            if b in self.last_w:
                deps.add(self.last_w[b])
            deps.update(self.readers.get(b, ()))
        return deps

    def _commit(self, me, reads, writes):
        for b in reads:
            self.readers.setdefault(b, []).append(me)
        for b in writes:
            self.last_w[b] = me
            self.readers[b] = []

    def op(self, eng, fn, reads=(), writes=()):
        self._wait(eng, self._deps(reads, writes))
        inst = fn(self.engs[eng])
        self.cnt[eng] += 1
        inst.then_inc(self.sem[eng], 1)
        self._commit((eng, self.cnt[eng]), reads, writes)

    def dma(self, q, out, in_, reads=(), writes=(), dsem=None):
        if dsem not in self.dsem:
            self.dsem[dsem] = self.nc.alloc_semaphore(name='d_' + str(len(self.dsem)))
            self.dcnt[dsem] = 0
        self._wait(q, self._deps(reads, writes))
        inst = self.engs[q].dma_start(out=out, in_=in_)
        self.dcnt[dsem] += 16
        inst.then_inc(self.dsem[dsem], 16)
        self._commit((dsem, self.dcnt[dsem]), reads, writes)

    def barrier(self):
        sp = self.engs['sp']
        for k in self.engs:
            if k != 'sp' and self.cnt[k]:
                sp.wait_ge(self.sem[k], self.cnt[k])
        for k, v in self.dcnt.items():
            sp.wait_ge(self.dsem[k], v)
        self.nbar += 1
        sp.sem_inc(self.bar, 1)
        for k in self.engs:
            if k != 'sp':
                self.engs[k].wait_ge(self.bar, self.nbar)
        self.last_w = {}
        self.readers = {}

    def finish(self):
        sp = self.engs['sp']
        for k in self.engs:
            if k != 'sp' and self.cnt[k]:
                sp.wait_ge(self.sem[k], self.cnt[k])
        for k, v in self.dcnt.items():
            sp.wait_ge(self.dsem[k], v)


class Arena:
    def __init__(self, big, base, limit):
        self.big = big
        self.off = base
        self.limit = limit

    def take(self, shape, dt):
        n = 1
        for s in shape[1:]:
            n *= s
        esz = 4 if dt == F32 else 2
        nbytes = n * esz
        off = self.off
        self.off += (nbytes + 63) // 64 * 64
        assert self.off <= self.limit, (self.off, self.limit)
        v = self.big[:, off // 2: off // 2 + nbytes // 2]
        if dt == F32:
            v = v.bitcast(F32)
        if len(shape) == 3:
            v = v.rearrange("p (a b) -> p a b", a=shape[1])
        elif len(shape) == 4:
            v = v.rearrange("p (a b c) -> p a b c", a=shape[1], b=shape[2])
        return v


PP_CW = 0
PP_CB = 32
PP_BM = 40
PP_KPOS = 56
PP_Q128 = 88
PP_Q256 = 104
PP_Q512 = 112
PP_FE = 116
PP_FO = 117
PP_N = 128

VB_MIX = 0
VB_DAG = 1024
VB_MLG = 2048
VB_MLP = 3072
VB_FIN = 4096
VB_BG = 5120
VB_LAM = 5128
VB_N = 5384

C_Q, C_K, C_V = 0, 1024, 2048
C_MQ, C_MK, C_MV, C_IF, C_MO, C_MG = 3072, 3584, 4096, 5120, 5128, 6152


def build(phases=5):
    nc = bass.Bass("TRN2", target_bir_lowering=False)
    xl = nc.dram_tensor("xl", [S, D], F32, kind="ExternalInput").ap()
    w_in = nc.dram_tensor("w_in", [D, DIN], F32, kind="ExternalInput").ap()
    w_a = nc.dram_tensor("w_a", [D, D], F32, kind="ExternalInput").ap()
    w_m = nc.dram_tensor("w_m", [D, D], F32, kind="ExternalInput").ap()
    w_o = nc.dram_tensor("w_o", [D, D], F32, kind="ExternalInput").ap()
    w_1 = nc.dram_tensor("w_1", [D, 4 * D], F32, kind="ExternalInput").ap()
    w_2 = nc.dram_tensor("w_2", [4 * D, D], F32, kind="ExternalInput").ap()
    vb = nc.dram_tensor("vb", [1, VB_N], F32, kind="ExternalInput").ap()
    ppd = nc.dram_tensor("pp", [128, PP_N], F32, kind="ExternalInput").ap()
    cst = nc.dram_tensor("cst", [128, 384], F32, kind="ExternalInput").ap()
    mkd = nc.dram_tensor("mk", [128, 8 * 512], F32, kind="ExternalInput").ap()
    kaugd = nc.dram_tensor("kaug", [4, S], F32, kind="ExternalInput").ap()
    qaugd = nc.dram_tensor("qaug", [8, 4, S // 2], F32, kind="ExternalInput").ap()
    out = nc.dram_tensor("out", [S // 2, D], F32, kind="ExternalOutput").ap()
    dbg_out = {}
    for name, shape in DBG.items():
        dbg_out[name] = nc.dram_tensor(name, list(shape), F32, kind="ExternalOutput").ap()

    P = Prog(nc)
    import contextlib
    with contextlib.ExitStack() as es:
        TOT = 205 * 1024
        big = es.enter_context(nc.sbuf_tensor("big", [128, TOT // 2], BF16))
        psum = es.enter_context(nc.psum_tensor("psum", [128, 8, 512], F32))

        def bank(i):
            return psum[:, i, :]

        def bankb(i):
            return psum[:, i, :].bitcast(BF16)

        CA = Arena(big, 0, 12800)
        identb = CA.take([128, 128], BF16)
        trib = CA.take([128, 128], BF16)
        trif = CA.take([128, 128], F32)
        onesf = CA.take([128, 128], F32)
        pp = CA.take([128, PP_N], F32)
        nqh = CA.take([128, 8, 28], F32)
        lamb = CA.take([128, 256], F32)
        small = CA.take([128, 64], F32)
        gbcA = CA.take([128, 1024], F32)
        gbcB = CA.take([128, 1024], F32)
        R1 = 12800
        R2 = R1 + 65536
        R3 = R2 + 32768
        T0 = R3 + 32768
        hT = Arena(big, R1, R2).take([128, 8, S], BF16)
        aT = Arena(big, R2, R3).take([128, 8, S // 2], BF16)
        mT = Arena(big, R3, T0).take([128, 8, S // 2], BF16)
        hT5 = hT.rearrange("p k (q e t) -> p k q e t", q=NPAIR, e=2)

        def dbg_store(name, src_ap, key, n=0):
            if name in dbg_out:
                P.dma('sp', dbg_out[name], src_ap, reads=[key], dsem='dbg_' + name)

        P.dma('pool', identb, cst[:, 0:128], writes=['identb'], dsem='c0')
        P.dma('pool', trib, cst[:, 128:256], writes=['trib'], dsem='c1')
        P.dma('sp', trif, cst[:, 128:256], writes=['trif'], dsem='c2')
        P.dma('sp', onesf, cst[:, 256:384], writes=['onesf'], dsem='c3')
        P.dma('sp', pp, ppd, writes=['pp'], dsem='c4')
        P.dma('sp', lamb, vb[:, VB_LAM:VB_LAM + 256].partition_broadcast(128), writes=['lamb'], dsem='c5')
        P.dma('sp', gbcA, vb[:, VB_MIX:VB_MIX + 1024].partition_broadcast(128), writes=['gbcA'], dsem='gA')
        P.op('dve', lambda e: e.tensor_tensor(out=lamb[:, 0:64], in0=lamb[:, 0:64], in1=lamb[:, 64:128], op=ALU.mult),
             reads=['lamb'], writes=['lamb'])
        P.op('dve', lambda e: e.tensor_tensor(out=lamb[:, 128:192], in0=lamb[:, 128:192], in1=lamb[:, 192:256], op=ALU.mult),
             reads=['lamb'], writes=['lamb'])
        P.op('dve', lambda e: e.tensor_reduce(out=small[:, 0:1], in_=lamb[:, 0:64], axis=mybir.AxisListType.X, op=ALU.add),
             reads=['lamb'], writes=['sm0'])
        P.op('dve', lambda e: e.tensor_reduce(out=small[:, 1:2], in_=lamb[:, 128:192], axis=mybir.AxisListType.X, op=ALU.add),
             reads=['lamb'], writes=['sm1'])
        P.op('act', lambda e: e.activation(out=small[:, 3:5], in_=small[:, 0:2], func=AF.Exp), reads=['sm0', 'sm1'], writes=['sm34'])
        P.op('dve', lambda e: e.scalar_tensor_tensor(out=small[:, 2:3], in0=small[:, 4:5], scalar=-LAM_INIT, in1=small[:, 3:4],
                                                     op0=ALU.add, op1=ALU.subtract), reads=['sm34'], writes=['neglam'])
        for h in range(8):
            P.op('dve', lambda e, h=h: e.tensor_scalar(out=nqh[:, h, :], in0=pp[:, PP_Q128:PP_Q128 + 28],
                                                       scalar1=-(2.0 ** -(h + 1)), scalar2=None, op0=ALU.mult),
                 reads=['pp'], writes=['nqh'])

        TA = Arena(big, T0, TOT)
        xin = [TA.take([128, 4, 1024], F32) for _ in range(2)]
        hb = [TA.take([128, 1024], BF16) for _ in range(4)]
        junk = TA.take([128, 1024], BF16)
        sc0 = TA.take([128, 16 * 8], F32)
        xl4 = xl.rearrange("(g t p) d -> g p t d", t=4, p=128)
        def p0_X(g):
            xs = g % 2
            xk = 'xin%d' % xs
            P.dma('sp' if xs == 0 else 'pool', xin[xs], xl4[g], writes=[xk], dsem=xk)
            c0 = 16 * g
            for tt in range(4):
                P.op('act', lambda e, tt=tt: e.activation(out=junk, in_=xin[xs][:, tt, :], func=AF.Square,
                                                          accum_out=sc0[:, c0 + tt:c0 + tt + 1]),
                     reads=[xk], writes=['junk', 'sc0a_%d_%d' % (g, tt)])
            P.op('act', lambda e: e.activation(out=sc0[:, c0 + 4:c0 + 8], in_=sc0[:, c0:c0 + 4], func=AF.Ln, scale=1.0 / D, bias=EPS),
                 reads=['sc0a_%d_%d' % (g, tt) for tt in range(4)], writes=['sc0b_%d' % g])
            P.op('act', lambda e: e.activation(out=sc0[:, c0 + 8:c0 + 12], in_=sc0[:, c0 + 4:c0 + 8], func=AF.Exp, scale=-0.5),
                 reads=['sc0b_%d' % g], writes=['sc0c_%d' % g])

        def p0_Y(g):
            xs = g % 2
            xk = 'xin%d' % xs
            c0 = 16 * g
            for tt in range(4):
                P.op('dve', lambda e, tt=tt: e.scalar_tensor_tensor(
                    out=hb[tt], in0=xin[xs][:, tt, :], scalar=sc0[:, c0 + 8 + tt:c0 + 9 + tt], in1=gbcA, op0=ALU.mult, op1=ALU.mult),
                    reads=[xk, 'sc0c_%d' % g, 'gbcA'], writes=['hb%d' % tt])
            for tt in range(4):
                bk = 4 * xs + tt

                def tr(e, tt=tt, bk=bk):
                    for kc in range(8):
                        ins = e.transpose(out=bankb(bk)[:, kc * 128:(kc + 1) * 128], in_=hb[tt][:, kc * 128:(kc + 1) * 128],
                                          identity=identb)
                    return ins
                P.op('pe', tr, reads=['hb%d' % tt, 'identb'], writes=['ps%d' % bk])
            for tt in range(4):
                bk = 4 * xs + tt
                i = g * 4 + tt
                if tt % 2:
                    P.op('act', lambda e, bk=bk, i=i: e.copy(out=hT[:, :, i * 128:(i + 1) * 128],
                                                            in_=bankb(bk).rearrange("p (k t) -> p k t", k=8)),
                         reads=['ps%d' % bk], writes=['hT'])
                else:
                    P.op('dve', lambda e, bk=bk, i=i: e.tensor_copy(out=hT[:, :, i * 128:(i + 1) * 128],
                                                                   in_=bankb(bk).rearrange("p (k t) -> p k t", k=8)),
                         reads=['ps%d' % bk], writes=['hT'])

        p0_X(0)
        for g in range(8):
            if g + 1 < 8:
                p0_X(g + 1)
            p0_Y(g)
        if 'd_hT' in dbg_out:
            dtmp = TA.take([128, 1024], F32)
            P.op('dve', lambda e: e.tensor_copy(out=dtmp.rearrange("p (k t) -> p k t", k=8), in_=hT[:, :, 0:128]), reads=['hT'], writes=['dtmp'])
            dbg_store('d_hT', dtmp, 'dtmp')
        P.barrier()
        if phases <= 0:
            P.finish()
            return nc

        import math

        def wslice(col0, n):
            return w_in[:, col0:col0 + n].rearrange("(kc p) f -> p kc f", p=128)

        fe = pp[:, PP_FE:PP_FE + 1]
        fo = pp[:, PP_FO:PP_FO + 1]
        MA = Arena(big, R2, R3)
        QTm2 = [MA.take([128, S], BF16) for _ in range(2)]
        KTm2 = [MA.take([128, S], BF16) for _ in range(2)]
        TA = Arena(big, T0, TOT)
        Uh = TA.take([128, NT, 131], F32)
        Uh5 = Uh.rearrange("p (q e) t -> p q e t", e=2)
        Wqk = [TA.take([128, 8, 128], BF16) for _ in range(2)]
        cacc2 = [TA.take([128, 4, 128], F32) for _ in range(2)]
        stt = [TA.take([128, 4, 128], F32) for _ in range(2)]
        Wv = [TA.take([128, 8, 256], BF16) for _ in range(2)]
        Wo = [TA.take([128, 8, 256], BF16) for _ in range(2)]
        Wif = TA.take([128, 8, 8], BF16)
        pre = TA.take([128, NT, 8], F32)
        SPt_f = TA.take([128, NT * 4], F32)
        SPp_f = TA.take([128, NT * 4], F32)
        As_f = TA.take([128, NT * 4], F32)
        Ct_f = TA.take([128, NT * 4], F32)
        Gd_f = TA.take([128, NPAIR * 4], F32)
        tmpg_f = TA.take([128, NT * 4], F32)
        SPs_f = TA.take([128, NPAIR * 4], F32)
        v3 = lambda a: a.rearrange("p (i h) -> p i h", h=4)
        v4 = lambda a: a.rearrange("p (q e h) -> p q e h", e=2, h=4)
        SPt, SPp, As, Ct, Gd, tmpg, SPs = v3(SPt_f), v3(SPp_f), v3(As_f), v3(Ct_f), v3(Gd_f), v3(tmpg_f), v3(SPs_f)
        bg = TA.take([128, 8], F32)
        Va = [TA.take([128, 2, 258], BF16) for _ in range(2)]
        Km = [TA.take([128, 2, 128], BF16) for _ in range(2)]
        MTb = [TA.take([128, 2, 128], BF16) for _ in range(2)]
        E = TA.take([128, 258], F32)
        Cbf = [TA.take([128, 258], BF16) for _ in range(2)]
        hpre = [TA.take([128, 256], F32) for _ in range(2)]
        og = [TA.take([128, 256], F32) for _ in range(2)]
        hn = TA.take([128, 256], F32)
        mtok = [TA.take([128, 256], BF16) for _ in range(2)]
        junkm = TA.take([128, 256], BF16)
        scm = [TA.take([128, 16], F32) for _ in range(2)]

        P.dma('pool', Wif, wslice(C_IF, 8), writes=['Wif'], dsem='wif')
        P.dma('sp', bg, vb[:, VB_BG:VB_BG + 8].partition_broadcast(128), writes=['bg'], dsem='bg')
        P.dma('sp', gbcB, vb[:, VB_MLG:VB_MLG + 1024].partition_broadcast(128), writes=['gbcB'], dsem='gB')

        def gmm(e):
            for i in range(NT):
                for kc in range(8):
                    ins = e.matmul(bank(0)[:, i * 8:(i + 1) * 8], lhsT=hT[:, kc, i * 128:(i + 1) * 128], rhs=Wif[:, kc, :],
                                   start=(kc == 0), stop=(kc == 7))
            return ins
        P.op('pe', gmm, reads=['hT', 'Wif'], writes=['ps0'])
        P.op('dve', lambda e: e.tensor_tensor(out=pre, in0=bank(0)[:, 0:256].rearrange("p (i g) -> p i g", g=8),
                                              in1=bg.unsqueeze(1).broadcast_to([128, NT, 8]), op=ALU.add),
             reads=['ps0', 'bg'], writes=['pre'])
        P.op('act', lambda e: e.activation(out=tmpg, in_=pre[:, :, 4:8], func=AF.Exp, scale=-1.0), reads=['pre'], writes=['tmpg'])
        P.op('act', lambda e: e.activation(out=SPt, in_=tmpg, func=AF.Ln, bias=1.0), reads=['tmpg'], writes=['SPt'])
        P.op('dve', lambda e: e.tensor_scalar(out=v4(SPp_f)[:, :, 0, :], in0=v4(SPt_f)[:, :, 1, :], scalar1=fe, scalar2=None, op0=ALU.mult),
             reads=['SPt', 'pp'], writes=['SPp0'])
        P.op('dve', lambda e: e.tensor_scalar(out=v4(SPp_f)[:, :, 1, :], in0=v4(SPt_f)[:, :, 0, :], scalar1=fo, scalar2=None, op0=ALU.mult),
             reads=['SPt', 'pp'], writes=['SPp1'])
        P.op('dve', lambda e: e.tensor_tensor(out=SPs, in0=v4(SPt_f)[:, :, 0, :], in1=v4(SPt_f)[:, :, 1, :], op=ALU.add),
             reads=['SPt'], writes=['SPs'])

        def cmm(e):
            e.matmul(bank(1)[:, 0:128], lhsT=trif, rhs=SPt_f, start=True, stop=False)
            e.matmul(bank(1)[:, 0:128], lhsT=onesf, rhs=SPp_f, start=False, stop=True)
            return e.matmul(bank(2)[:, 0:64], lhsT=onesf, rhs=SPs_f, start=True, stop=True)
        P.op('pe', cmm, reads=['trif', 'onesf', 'SPt', 'SPp0', 'SPp1', 'SPs'], writes=['ps1', 'ps2'])
        psB = bank(1)[:, 0:128].rearrange("p (i h) -> p i h", h=4)
        psG = bank(2)[:, 0:64].rearrange("p (i h) -> p i h", h=4)
        P.op('dve', lambda e: e.tensor_tensor(out=tmpg, in0=pre[:, :, 0:4], in1=psB, op=ALU.add), reads=['pre', 'ps1', 'SPt'], writes=['tmpg'])
        P.op('act', lambda e: e.activation(out=As, in_=tmpg, func=AF.Exp), reads=['tmpg'], writes=['As'])
        P.op('act', lambda e: e.activation(out=Ct, in_=psB, func=AF.Exp, scale=-1.0, bias=math.log(128.0 ** -0.5)), reads=['ps1'], writes=['Ct'])
        P.op('act', lambda e: e.activation(out=Gd, in_=psG, func=AF.Exp, scale=-1.0), reads=['ps2'], writes=['Gd'])
        dbg_store('d_gates', As_f, 'As')
        dbg_store('d_ct', Ct_f, 'Ct')
        dbg_store('d_gd', Gd_f, 'Gd')

        qkn = [0]

        def qk_load(h):
            for wi, col0 in enumerate((C_MQ + h * 128, C_MK + h * 128)):
                wj = wi
                P.dma('pool', Wqk[wj], wslice(col0, 128), writes=['Wqk%d' % wj], dsem='wqk%d' % wj)

        def qk_block(h, wi, tb):
            wj = wi
            wk = 'Wqk%d' % wj
            cc = h if wi == 0 else 4 + h
            dst = (QTm2 if wi == 0 else KTm2)[h % 2]
            dk = ('QTm%d' if wi == 0 else 'KTm%d') % (h % 2)
            n_ = qkn[0]
            qkn[0] += 1
            s2 = n_ % 2

            def pm(e):
                for kc in range(8):
                    ins = e.matmul(bank(7), lhsT=Wqk[wj][:, kc, :], rhs=hT[:, kc, tb * 512:(tb + 1) * 512],
                                   start=(kc == 0), stop=(kc == 7))
                return ins
            P.op('pe', pm, reads=['hT', wk], writes=['ps7'])
            return (h, wi, tb, n_)

        def qk_blockA(ptok):
            h, wi, tb, n_ = ptok
            cc = h if wi == 0 else 4 + h
            dst = (QTm2 if wi == 0 else KTm2)[h % 2]
            dk = ('QTm%d' if wi == 0 else 'KTm%d') % (h % 2)
            s2 = n_ % 2
            uk = 'Uh%d' % tb
            ukp = ['Uh%d' % (tb - 1)] if tb > 0 else []
            if False:
                pass
            else:
                P.op('act', lambda e: e.copy(out=Uh[:, 4 * tb:4 * tb + 4, 3:131], in_=bank(7).rearrange("p (a t) -> p a t", a=4)),
                     reads=['ps7'], writes=[uk])
            p0, p1 = 2 * tb, 2 * tb + 2
            hk = 'UhH%d' % tb
            if tb == 0:
                P.op('dve', lambda e: e.memset(Uh5[:, 0:1, 0, 0:3], 0.0), reads=[], writes=[hk])
                P.op('dve', lambda e: e.tensor_scalar(out=Uh5[:, 1:2, 0, 0:3], in0=Uh5[:, 0:1, 1, 128:131], scalar1=fo, scalar2=None,
                                                      op0=ALU.mult), reads=[uk, 'pp', hk], writes=[hk])
            else:
                P.op('dve', lambda e: e.tensor_scalar(out=Uh5[:, p0:p1, 0, 0:3], in0=Uh5[:, p0 - 1:p1 - 1, 1, 128:131], scalar1=fo, scalar2=None,
                                                      op0=ALU.mult), reads=[uk, 'pp'] + ukp, writes=[hk])
            P.op('dve', lambda e: e.scalar_tensor_tensor(out=Uh5[:, p0:p1, 0, 0:3], in0=Uh5[:, p0:p1, 1, 128:131], scalar=fe,
                                                         in1=Uh5[:, p0:p1, 0, 0:3], op0=ALU.mult, op1=ALU.add),
                 reads=[uk, hk], writes=[hk])
            P.op('dve', lambda e: e.tensor_scalar(out=Uh5[:, p0:p1, 1, 0:3], in0=Uh5[:, p0:p1, 0, 128:131], scalar1=fo, scalar2=None,
                                                  op0=ALU.mult), reads=[uk, hk], writes=[hk])
            if tb == 0:
                P.op('dve', lambda e: e.scalar_tensor_tensor(out=Uh5[:, 1:2, 1, 0:3], in0=Uh5[:, 0:1, 0, 128:131], scalar=fe,
                                                             in1=Uh5[:, 1:2, 1, 0:3], op0=ALU.mult, op1=ALU.add),
                     reads=[uk, hk], writes=[hk])
            else:
                P.op('dve', lambda e: e.scalar_tensor_tensor(out=Uh5[:, p0:p1, 1, 0:3], in0=Uh5[:, p0 - 1:p1 - 1, 0, 128:131], scalar=fe,
                                                             in1=Uh5[:, p0:p1, 1, 0:3], op0=ALU.mult, op1=ALU.add),
                     reads=[uk, hk] + ukp, writes=[hk])
            wcol = lambda j: pp[:, PP_CW + cc * 4 + j:PP_CW + cc * 4 + j + 1]
            bcol = pp[:, PP_CB + cc:PP_CB + cc + 1]
            ca = cacc2[s2]
            ck = 'cacc%d' % s2
            ub = Uh[:, 4 * tb:4 * tb + 4, :]
            P.op('dve', lambda e: e.tensor_scalar(out=ca, in0=ub[:, :, 3:131], scalar1=wcol(3), scalar2=bcol, op0=ALU.mult, op1=ALU.add),
                 reads=[uk, hk, 'pp'], writes=[ck])
            for j in (2, 1, 0):
                P.op('dve', lambda e, j=j: e.scalar_tensor_tensor(out=ca, in0=ub[:, :, j:j + 128], scalar=wcol(j), in1=ca,
                                                                  op0=ALU.mult, op1=ALU.add), reads=[uk, hk, ck], writes=[ck])
            return (s2, dst, dk, tb)

        def qk_blockB(tok):
            s2, dst, dk, tb = tok
            ca = cacc2[s2]
            ck = 'cacc%d' % s2
            st_ = stt[s2]
            sk_ = 'stt%d' % s2
            P.op('act', lambda e: e.activation(out=st_, in_=ca, func=AF.Exp, scale=-1.0), reads=[ck], writes=[sk_])
            P.op('act', lambda e: e.activation(out=st_, in_=st_, func=AF.Ln, bias=1.0), reads=[sk_], writes=[sk_])
            P.op('act', lambda e: e.activation(out=st_, in_=st_, func=AF.Exp, scale=-1.0), reads=[sk_], writes=[sk_])
            P.op('dve', lambda e: e.tensor_tensor(out=dst[:, tb * 512:(tb + 1) * 512].rearrange("p (a t) -> p a t", a=4), in0=ca, in1=st_,
                                                  op=ALU.mult), reads=[ck, sk_], writes=[dk])

        def stageA(h, p, sl):
            QTm, KTm = QTm2[h % 2], KTm2[h % 2]
            qk_, kk_ = 'QTm%d' % (h % 2), 'KTm%d' % (h % 2)
            wv, wo = 'Wv%d' % (h % 2), 'Wo%d' % (h % 2)

            def vm(e):
                for j in range(2):
                    i = 2 * p + j
                    for kc in range(8):
                        ins = e.matmul(bank(0)[:, j * 256:(j + 1) * 256], lhsT=hT[:, kc, i * 128:(i + 1) * 128], rhs=Wv[h % 2][:, kc, :],
                                       start=(kc == 0), stop=(kc == 7))
                return ins
            P.op('pe', vm, reads=['hT', wv], writes=['ps0'])
            for j in range(2):
                i = 2 * p + j
                P.op('act', lambda e, j=j, i=i: e.activation(out=Va[sl][:, j, 0:256], in_=bank(0)[:, j * 256:(j + 1) * 256], func=AF.Copy,
                                                             scale=As[:, i, h:h + 1]), reads=['ps0', 'As'], writes=['Va%d_%d' % (sl, j)])
            P.op('dve', lambda e: e.tensor_copy(out=Va[sl][:, :, 256:257], in_=As[:, 2 * p:2 * p + 2, h:h + 1]),
                 reads=['As'], writes=['Va%d_o' % sl])

            def kt(e):
                for j in range(2):
                    i = 2 * p + j
                    ins = e.transpose(out=bankb(1)[:, j * 128:(j + 1) * 128], in_=KTm[:, i * 128:(i + 1) * 128], identity=identb)
                return ins
            P.op('pe', kt, reads=[kk_, 'identb'], writes=['ps1'])
            P.op('act', lambda e: e.copy(out=Km[sl], in_=bankb(1)[:, 0:256].rearrange("p (j t) -> p j t", j=2)),
                 reads=['ps1'], writes=['Km%d' % sl])

            def mtm(e):
                for j in range(2):
                    ins = e.matmul(bank(2)[:, j * 128:(j + 1) * 128], lhsT=KTm[:, (2 * p + j) * 128:(2 * p + j + 1) * 128],
                                   rhs=QTm[:, 2 * p * 128:(2 * p + 1) * 128], start=True, stop=True)
                return ins
            P.op('pe', mtm, reads=[kk_, qk_], writes=['ps2'])
            P.op('dve', lambda e: e.tensor_tensor(out=MTb[sl][:, 0, :], in0=bank(2)[:, 0:128], in1=trib, op=ALU.mult),
                 reads=['ps2', 'trib'], writes=['MTb%d_0' % sl])
            P.op('dve', lambda e: e.tensor_scalar(out=MTb[sl][:, 1, :], in0=bank(2)[:, 128:256], scalar1=fe, scalar2=None, op0=ALU.mult),
                 reads=['ps2', 'pp'], writes=['MTb%d_1' % sl])

            def om(e):
                for kc in range(8):
                    ins = e.matmul(bank(3)[:, 0:256], lhsT=hT[:, kc, 2 * p * 128:(2 * p + 1) * 128], rhs=Wo[h % 2][:, kc, :],
                                   start=(kc == 0), stop=(kc == 7))
                return ins
            P.op('pe', om, reads=['hT', wo], writes=['ps3'])
            P.op('act', lambda e: e.activation(out=og[sl], in_=bank(3)[:, 0:256], func=AF.Exp, scale=-1.0), reads=['ps3'], writes=['og%d' % sl])
            P.op('act', lambda e: e.activation(out=og[sl], in_=og[sl], func=AF.Ln, bias=1.0), reads=['og%d' % sl], writes=['og%d' % sl])
            P.op('act', lambda e: e.activation(out=og[sl], in_=og[sl], func=AF.Exp, scale=-1.0), reads=['og%d' % sl], writes=['og%d' % sl])

        def stageB(h, p, sl):
            QTm, KTm = QTm2[h % 2], KTm2[h % 2]
            qk_, kk_ = 'QTm%d' % (h % 2), 'KTm%d' % (h % 2)
            vak = ['Va%d_0' % sl, 'Va%d_1' % sl, 'Va%d_o' % sl]

            def am(e):
                e.matmul(bank(4)[:, 0:257], lhsT=MTb[sl][:, 0, :], rhs=Va[sl][:, 0, 0:257], start=True, stop=False)
                ins = e.matmul(bank(4)[:, 0:257], lhsT=MTb[sl][:, 1, :], rhs=Va[sl][:, 1, 0:257], start=False, stop=(p == 0))
                if p > 0:
                    ins = e.matmul(bank(4)[:, 0:257], lhsT=QTm[:, 2 * p * 128:(2 * p + 1) * 128], rhs=Cbf[(p - 1) % 2][:, 0:257],
                                   start=False, stop=True)
                return ins
            P.op('pe', am, reads=['MTb%d_0' % sl, 'MTb%d_1' % sl, qk_, 'Cbf%d' % ((p - 1) % 2)] + vak, writes=['ps4'])
            if p < NPAIR - 1:
                def um(e):
                    e.matmul(bank(5)[:, 0:257], lhsT=Km[sl][:, 0, :], rhs=Va[sl][:, 0, 0:257], start=True, stop=False)
                    return e.matmul(bank(5)[:, 0:257], lhsT=Km[sl][:, 1, :], rhs=Va[sl][:, 1, 0:257], start=False, stop=True)
                P.op('pe', um, reads=['Km%d' % sl] + vak, writes=['ps5'])
                if p == 0:
                    P.op('dve', lambda e: e.tensor_copy(out=E[:, 0:257], in_=bank(5)[:, 0:257]), reads=['ps5'], writes=['E'])
                else:
                    P.op('dve', lambda e: e.scalar_tensor_tensor(out=E[:, 0:257], in0=E[:, 0:257], scalar=Gd[:, p - 1, h:h + 1],
                                                                 in1=bank(5)[:, 0:257], op0=ALU.mult, op1=ALU.add),
                         reads=['ps5', 'E', 'Gd'], writes=['E'])
                P.op('act', lambda e: e.activation(out=Cbf[p % 2][:, 0:257], in_=E[:, 0:257], func=AF.Copy, scale=Gd[:, p, h:h + 1]),
                     reads=['E', 'Gd'], writes=['Cbf%d' % (p % 2)])
            c = Ct[:, 2 * p, h:h + 1]
            s_ = scm[sl]
            sk = 'scm%d' % sl
            P.op('dve', lambda e: e.tensor_tensor(out=s_[:, 0:1], in0=bank(4)[:, 256:257], in1=c, op=ALU.mult),
                 reads=['ps4', 'Ct'], writes=[sk + 'a'])
            P.op('dve', lambda e: e.scalar_tensor_tensor(out=s_[:, 7:8], in0=s_[:, 0:1], scalar=-1.0, in1=s_[:, 0:1],
                                                         op0=ALU.mult, op1=ALU.max), reads=[sk + 'a'], writes=[sk + 'h'])
            P.op('dve', lambda e: e.tensor_scalar(out=s_[:, 1:2], in0=s_[:, 7:8], scalar1=1.0, scalar2=None, op0=ALU.max),
                 reads=[sk + 'h'], writes=[sk + 'b'])
            P.op('dve', lambda e: e.reciprocal(out=s_[:, 2:3], in_=s_[:, 1:2]), reads=[sk + 'b'], writes=[sk + 'c'])
            P.op('dve', lambda e: e.tensor_tensor(out=s_[:, 3:4], in0=s_[:, 2:3], in1=c, op=ALU.mult), reads=[sk + 'c', 'Ct'], writes=[sk + 'd'])
            P.op('dve', lambda e: e.tensor_scalar(out=hpre[sl], in0=bank(4)[:, 0:256], scalar1=s_[:, 3:4], scalar2=None, op0=ALU.mult),
                 reads=['ps4', sk + 'd'], writes=['hpre%d' % sl])
            P.op('act', lambda e: e.activation(out=junkm, in_=hpre[sl], func=AF.Square, accum_out=s_[:, 4:5]),
                 reads=['hpre%d' % sl], writes=['junkm', sk + 'e'])
            P.op('act', lambda e: e.activation(out=s_[:, 5:6], in_=s_[:, 4:5], func=AF.Ln, scale=1.0 / 256, bias=EPS),
                 reads=[sk + 'e'], writes=[sk + 'f'])
            P.op('act', lambda e: e.activation(out=s_[:, 6:7], in_=s_[:, 5:6], func=AF.Exp, scale=-0.5), reads=[sk + 'f'], writes=[sk + 'g'])
            P.op('dve', lambda e: e.scalar_tensor_tensor(out=hn, in0=hpre[sl], scalar=s_[:, 6:7], in1=gbcB[:, h * 256:(h + 1) * 256],
                                                         op0=ALU.mult, op1=ALU.mult), reads=['hpre%d' % sl, sk + 'g', 'gbcB'], writes=['hn'])
            P.op('dve', lambda e: e.tensor_tensor(out=mtok[sl], in0=hn, in1=og[sl], op=ALU.mult), reads=['hn', 'og%d' % sl], writes=['mtok%d' % sl])


        def stageB2(h, p, sl):
            def tm(e):
                for j in range(2):
                    ins = e.transpose(out=bankb(6)[:, j * 128:(j + 1) * 128], in_=mtok[sl][:, j * 128:(j + 1) * 128], identity=identb)
                return ins
            P.op('pe', tm, reads=['mtok%d' % sl, 'identb'], writes=['ps6'])
            P.op('act', lambda e: e.copy(out=mT[:, 2 * h:2 * h + 2, p * 128:(p + 1) * 128],
                                        in_=bankb(6)[:, 0:256].rearrange("p (j t) -> p j t", j=2)), reads=['ps6'], writes=['mT'])

        qk_load(0)
        for wi in range(2):
            for tb in range(8):
                qk_blockB(qk_blockA(qk_block(0, wi, tb)))
        for h in range(4):
            P.dma('pool', Wv[h % 2], wslice(C_MV + h * 256, 256), writes=['Wv%d' % (h % 2)], dsem='wv%d' % (h % 2))
            P.dma('pool', Wo[h % 2], wslice(C_MO + h * 256, 256), writes=['Wo%d' % (h % 2)], dsem='wo%d' % (h % 2))
            blocks = []
            if h + 1 < 4:
                qk_load(h + 1)
                blocks = [(wi, tb) for wi in range(2) for tb in range(8)]
            tok = None
            stageA(h, 0, 0)
            for p in range(NPAIR):
                if p + 1 < NPAIR:
                    stageA(h, p + 1, (p + 1) % 2)
                ptok = qk_block(h + 1, *blocks[p]) if blocks else None
                stageB(h, p, p % 2)
                if p >= 1:
                    stageB2(h, p - 1, (p - 1) % 2)
                ntok = qk_blockA(ptok) if ptok is not None else None
                if tok is not None:
                    qk_blockB(tok)
                tok = ntok
            stageB2(h, NPAIR - 1, (NPAIR - 1) % 2)
            if tok is not None:
                qk_blockB(tok)
        if 'd_mT' in dbg_out:
            P.op('dve', lambda e: e.tensor_copy(out=gbcA.rearrange("p (k t) -> p k t", k=8), in_=mT[:, :, 1920:2048]), reads=['mT'], writes=['gbcA'])
            dbg_store('d_mT', gbcA, 'gbcA')
        P.barrier()
        if phases <= 1:
            P.finish()
            return nc

        TA = Arena(big, T0, TOT)
        KTa = TA.take([128, S], BF16)
        KTb = TA.take([128, S], BF16)
        Vg = TA.take([128, NT, 130], BF16)
        QTa = TA.take([128, S // 2], BF16)
        QTb = TA.take([128, S // 2], BF16)
        Pb = [[TA.take([128, 512], BF16) for _ in range(2)] for _ in range(2)]
        maskb = TA.take([128, 8, 512], BF16)
        Wd = [[TA.take([128, 8, 128], BF16) for _ in range(3)] for _ in range(2)]
        otmp = [TA.take([128, 128], F32) for _ in range(4)]
        ofin = [TA.take([128, 128], F32) for _ in range(4)]
        abf = [TA.take([128, 128], BF16) for _ in range(4)]
        junka = TA.take([128, 128], BF16)
        sca = [TA.take([128, 16], F32) for _ in range(4)]

        P.dma('sp', gbcA, vb[:, VB_DAG:VB_DAG + 1024].partition_broadcast(128), writes=['gbcA'], dsem='gA')
        P.op('dve', lambda e: e.tensor_scalar(out=gbcA, in0=gbcA, scalar1=1.0 - LAM_INIT, scalar2=None, op0=ALU.mult),
             reads=['gbcA'], writes=['gbcA'])
        P.op('dve', lambda e: e.memset(Vg[:, :, 128:129], 1.0), reads=[], writes=['Vg1'])
        P.op('dve', lambda e: e.memset(KTa[64:128, :], 0.0), reads=[], writes=['KTaug_a'])
        P.op('dve', lambda e: e.memset(KTb[0:64, :], 0.0), reads=[], writes=['KTaug_b'])
        P.op('dve', lambda e: e.memset(QTa[64:128, :], 0.0), reads=[], writes=['QTaug_a'])
        P.op('dve', lambda e: e.memset(QTb[0:64, :], 0.0), reads=[], writes=['QTaug_b'])
        P.dma('pool', KTa[64:68, :], kaugd, writes=['KTaug_a'], dsem='kaug_a')
        P.dma('pool', KTb[0:4, :], kaugd, writes=['KTaug_b'], dsem='kaug_b')

        def load_wd(h):
            sl = h % 2
            for j, c0 in enumerate((C_Q, C_K, C_V)):
                P.dma('pool', Wd[sl][j], wslice(c0 + h * 128, 128), writes=['Wd%d_%d' % (sl, j)], dsem='wd%d_%d' % (sl, j))

        cpy_i = [0]
        PB = [7, 0, 1, 2, 3] if FLAGS['pb'] else [7, 7, 7, 7, 7]
        dve_only = [FLAGS['dveonly']]

        def evac(out_ap, in_ap, rk, wk):
            cpy_i[0] += 1
            if cpy_i[0] % 2 or dve_only[0]:
                P.op('dve', lambda e: e.tensor_copy(out=out_ap, in_=in_ap), reads=rk, writes=wk)
            else:
                P.op('act', lambda e: e.copy(out=out_ap, in_=in_ap), reads=rk, writes=wk)

        def proj_k(h):
            sl = h % 2
            for tb in range(8):
                bk = PB[tb % 5]

                def pm(e, tb=tb, bk=bk):
                    for kc in range(8):
                        ins = e.matmul(bank(bk), lhsT=Wd[sl][1][:, kc, :], rhs=hT[:, kc, tb * 512:(tb + 1) * 512],
                                       start=(kc == 0), stop=(kc == 7))
                    return ins
                P.op('pe', pm, reads=['hT', 'Wd%d_1' % sl], writes=['ps%d' % bk])
                evac(KTa[0:64, tb * 512:(tb + 1) * 512], bank(bk)[0:64, :], ['ps%d' % bk], ['KTa%d' % tb])
                evac(KTb[64:128, tb * 512:(tb + 1) * 512], bank(bk)[64:128, :], ['ps%d' % bk], ['KTb%d' % tb])

        def proj_qv(h):
            sl = h % 2
            P.dma('pool', QTa[64:68, :], qaugd[h], writes=['QTaug_a'], dsem='qaug_a')
            P.dma('pool', QTb[0:4, :], qaugd[h], writes=['QTaug_b'], dsem='qaug_b')
            for m in range(4):
                bk = PB[(m + 3) % 5]

                def pm(e, m=m, bk=bk):
                    for kc in range(8):
                        ins = e.matmul(bank(bk), lhsT=Wd[sl][0][:, kc, :], rhs=hT5[:, kc, 4 * m:4 * m + 4, 0, :],
                                       start=(kc == 0), stop=(kc == 7))
                    return ins
                P.op('pe', pm, reads=['hT', 'Wd%d_0' % sl], writes=['ps%d' % bk])
                evac(QTa[0:64, m * 512:(m + 1) * 512], bank(bk)[0:64, :], ['ps%d' % bk], ['QTa%d' % m])
                evac(QTb[64:128, m * 512:(m + 1) * 512], bank(bk)[64:128, :], ['ps%d' % bk], ['QTb%d' % m])
            for i4 in range(8):
                bk = PB[(i4 + 2) % 5]

                def vm(e, i4=i4, bk=bk):
                    for j in range(4):
                        i = 4 * i4 + j
                        for kc in range(8):
                            ins = e.matmul(bank(bk)[:, j * 128:(j + 1) * 128], lhsT=hT[:, kc, i * 128:(i + 1) * 128],
                                           rhs=Wd[sl][2][:, kc, :], start=(kc == 0), stop=(kc == 7))
                    return ins
                P.op('pe', vm, reads=['hT', 'Wd%d_2' % sl], writes=['ps%d' % bk])
                evac(Vg[:, 4 * i4:4 * i4 + 4, 0:128], bank(bk).rearrange("p (j d) -> p j d", j=4), ['ps%d' % bk], ['Vg%d' % i4])

        def smm(h, m, i, a):
            general = i >= 8 * m
            c0 = ((i - 8 * m) // 2) * 128 if (general and FLAGS['trimmm']) else 0
            KTx = KTb if a else KTa
            QTx = QTb if a else QTa
            ab = 'b' if a else 'a'
            bk = (2 * (i % 2) + a) if FLAGS['dbl'] else a

            def f(e):
                ins = e.matmul(bank(bk)[:, c0:512], lhsT=KTx[:, i * 128:(i + 1) * 128], rhs=QTx[:, m * 512 + c0:(m + 1) * 512],
                               start=True, stop=(not general))
                if general:
                    if FLAGS['mask128']:
                        u0 = (i - 8 * m) // 2
                        ins = e.matmul(bank(bk)[:, u0 * 128:(u0 + 1) * 128], lhsT=identb, rhs=maskb[:, i - 8 * m, u0 * 128:(u0 + 1) * 128],
                                       start=False, stop=True)
                    else:
                        ins = e.matmul(bank(bk)[:, c0:512], lhsT=identb, rhs=maskb[:, i - 8 * m, c0:512], start=False, stop=True)
                return ins
            P.op('pe', f, reads=['KT%s%d' % (ab, i // 4), 'QT%s%d' % (ab, m), 'KTaug_' + ab, 'QTaug_' + ab, 'maskb', 'identb'],
                 writes=['ps%d' % bk])

        def expo(h, m, i, a, bs):
            general = i >= 8 * m
            c0 = ((i - 8 * m) // 2) * 128 if (general and FLAGS['trim']) else 0
            sl = i % 2
            bk = (2 * (i % 2) + a) if FLAGS['dbl'] else a
            P.op('act', lambda e: e.activation(out=Pb[a][sl][:, c0:512], in_=bank(bk)[:, c0:512], func=AF.Exp, scale=0.125),
                 reads=['ps%d' % bk], writes=['Pb%d_%d' % (a, sl)])

        def oreg(a, u):
            if u < 3:
                return bank(4 + a)[:, u * 129:(u + 1) * 129], 'ps%d' % (4 + a)
            return bank(6)[:, a * 129:(a + 1) * 129], 'ps6'

        def omm(h, m, i, a):
            general = i >= 8 * m
            r = i - 8 * m
            sl = i % 2
            us = [u for u in range(4) if not (general and (r // 2 > u))]

            def f(e):
                for u in us:
                    o_, _ = oreg(a, u)
                    last = (i == 8 * m + 2 * u + 1)
                    ins = e.matmul(o_, lhsT=Pb[a][sl][:, u * 128:(u + 1) * 128], rhs=Vg[:, i, 0:129],
                                   start=(i == 0 and (u == 0 or (u == 3 and a == 0))), stop=last, skip_group_check=True)
                return ins
            P.op('pe', f, reads=['Pb%d_%d' % (a, sl), 'Vg%d' % (i // 4), 'Vg1'], writes=['ps%d' % (4 + a), 'ps6'])

        def post1(h, m):
            regs = [(oreg(0, u), oreg(1, u)) for u in range(4)]
            for u in range(4):
                (O0u, k0), (O1u, k1) = regs[u]
                s_ = sca[u]
                sk = 'sca%d' % u
                P.op('dve', lambda e, s_=s_, O0u=O0u: e.reciprocal(out=s_[:, 0:1], in_=O0u[:, 128:129]), reads=[k0], writes=[sk + 'a'])
                P.op('dve', lambda e, s_=s_, O1u=O1u: e.reciprocal(out=s_[:, 1:2], in_=O1u[:, 128:129]), reads=[k1], writes=[sk + 'b'])
            for u in range(4):
                s_ = sca[u]
                sk = 'sca%d' % u
                P.op('dve', lambda e, s_=s_: e.tensor_tensor(out=s_[:, 2:3], in0=s_[:, 1:2], in1=small[:, 2:3], op=ALU.mult),
                     reads=[sk + 'b', 'neglam'], writes=[sk + 'c'])
            for u in range(4):
                (O0u, k0), (O1u, k1) = regs[u]
                s_ = sca[u]
                sk = 'sca%d' % u
                P.op('dve', lambda e, s_=s_, O1u=O1u, u=u: e.tensor_scalar(out=otmp[u], in0=O1u[:, 0:128], scalar1=s_[:, 2:3], scalar2=None,
                                                                           op0=ALU.mult), reads=[k1, sk + 'c'], writes=['otmp%d' % u])
            for u in range(4):
                (O0u, k0), (O1u, k1) = regs[u]
                s_ = sca[u]
                sk = 'sca%d' % u
                P.op('dve', lambda e, s_=s_, O0u=O0u, u=u: e.scalar_tensor_tensor(out=ofin[u], in0=O0u[:, 0:128], scalar=s_[:, 0:1], in1=otmp[u],
                                                                                  op0=ALU.mult, op1=ALU.add),
                     reads=[k0, sk + 'a', 'otmp%d' % u], writes=['ofin%d' % u])

        def post1b(h, m):
            for u in range(4):
                s_ = sca[u]
                sk = 'sca%d' % u
                P.op('act', lambda e, s_=s_, u=u: e.activation(out=junka, in_=ofin[u], func=AF.Square, accum_out=s_[:, 3:4]),
                     reads=['ofin%d' % u], writes=['junka', sk + 'd'])
                P.op('act', lambda e, s_=s_: e.activation(out=s_[:, 4:5], in_=s_[:, 3:4], func=AF.Ln, scale=1.0 / 128, bias=EPS),
                     reads=[sk + 'd'], writes=[sk + 'e'])
                P.op('act', lambda e, s_=s_: e.activation(out=s_[:, 5:6], in_=s_[:, 4:5], func=AF.Exp, scale=-0.5),
                     reads=[sk + 'e'], writes=[sk + 'f'])
                P.op('dve', lambda e, s_=s_, u=u: e.scalar_tensor_tensor(out=abf[u], in0=ofin[u], scalar=s_[:, 5:6],
                                                                         in1=gbcA[:, h * 128:(h + 1) * 128], op0=ALU.mult, op1=ALU.mult),
                     reads=['ofin%d' % u, sk + 'f', 'gbcA'], writes=['abf%d' % u])

        def post2(h, m):
            def tm(e):
                for u in range(4):
                    ins = e.transpose(out=bankb(7)[:, u * 128:(u + 1) * 128], in_=abf[u], identity=identb)
                return ins
            P.op('pe', tm, reads=['abf0', 'abf1', 'abf2', 'abf3', 'identb'], writes=['ps7'])
            evac(aT[:, h, m * 512:(m + 1) * 512], bankb(7)[:, 0:512], ['ps7'], ['aT'])

        pending = []
        pend_b = []
        load_wd(0)
        P.dma('pool', maskb, mkd.rearrange("p (r q) -> p r q", r=8), writes=['maskb'], dsem='maskb')
        bsn = [0]
        for h in range(8):
            if h + 1 < 8:
                load_wd(h + 1)
            proj_k(h)
            while pend_b:
                post1b(*pend_b.pop(0))
            while pending:
                post2(*pending.pop(0))
            proj_qv(h)
            for m in range(4):
                nt = 8 * m + 8
                bs = 0
                smm(h, m, 0, 0)
                smm(h, m, 0, 1)
                for i in range(nt):
                    expo(h, m, i, 0, bs)
                    expo(h, m, i, 1, bs)
                    if i + 1 < nt:
                        smm(h, m, i + 1, 0)
                        smm(h, m, i + 1, 1)
                    omm(h, m, i, 0)
                    omm(h, m, i, 1)
                    if i == 1:
                        while pend_b:
                            post1b(*pend_b.pop(0))
                    if i == 3:
                        while pending:
                            post2(*pending.pop(0))
                post1(h, m)
                pend_b.append((h, m))
                pending.append((h, m))
        while pend_b:
            post1b(*pend_b.pop(0))
        while pending:
            post2(*pending.pop(0))
        for nm, lo, n in (('d_aT', 0, 128), ('d_aT2', 1920, 128)):
            if nm in dbg_out:
                P.op('dve', lambda e, lo=lo, n=n: e.tensor_copy(out=gbcB.rearrange("p (k t) -> p k t", k=8), in_=aT[:, :, lo:lo + n]),
                     reads=['aT'], writes=['gbcB'])
                dbg_store(nm, gbcB, 'gbcB')
        dve_only[0] = False
        P.barrier()
        if phases <= 2:
            P.finish()
            return nc

        TA = Arena(big, T0, TOT)
        mgT = TA.take([128, 8, S // 2], BF16)
        Wm = [[TA.take([128, 8, 128], BF16) for _ in range(4)] for _ in range(2)]
        sig = [[TA.take([128, 512], F32) for _ in range(2)] for _ in range(2)]
        t1 = [TA.take([128, 512], F32) for _ in range(2)]
        t2 = [TA.take([128, 512], F32) for _ in range(2)]

        def wsq(w, c0, n):
            return w[:, c0:c0 + n].rearrange("(kc p) f -> p kc f", p=128)

        def load_wm(dc):
            sl = dc % 2
            srcs = (wslice(C_MG + dc * 128, 128), wslice(C_MG + 1024 + dc * 128, 128), wsq(w_a, dc * 128, 128), wsq(w_m, dc * 128, 128))
            for j, src in enumerate(srcs):
                P.dma('pool', Wm[sl][j], src, writes=['Wm%d_%d' % (sl, j)], dsem='wm%d_%d' % (sl, j))

        load_wm(0)
        it = 0
        for dc in range(8):
            if dc + 1 < 8:
                load_wm(dc + 1)
            sl = dc % 2
            for m in range(4):
                b0 = 4 * (it % 2)
                s2 = it % 2
                it += 1
                rhss = (lambda kc, m=m: hT5[:, kc, 4 * m:4 * m + 4, 0, :], lambda kc, m=m: hT5[:, kc, 4 * m:4 * m + 4, 0, :],
                        lambda kc, m=m: aT[:, kc, m * 512:(m + 1) * 512], lambda kc, m=m: mT[:, kc, m * 512:(m + 1) * 512])
                rkeys = ('hT', 'hT', 'aT', 'mT')
                for j in range(4):
                    def pm(e, j=j, b0=b0):
                        for kc in range(8):
                            ins = e.matmul(bank(b0 + j), lhsT=Wm[sl][j][:, kc, :], rhs=rhss[j](kc), start=(kc == 0), stop=(kc == 7))
                        return ins
                    P.op('pe', pm, reads=[rkeys[j], 'Wm%d_%d' % (sl, j)], writes=['ps%d' % (b0 + j)])
                for j in range(2):
                    bcol = pp[:, PP_BM + 8 * j + dc:PP_BM + 8 * j + dc + 1]
                    P.op('act', lambda e, j=j, b0=b0, bcol=bcol, s2=s2: e.activation(out=sig[s2][j], in_=bank(b0 + j), func=AF.Sigmoid, bias=bcol),
                         reads=['ps%d' % (b0 + j), 'pp'], writes=['sig%d_%d' % (s2, j)])
                P.op('dve', lambda e, b0=b0, s2=s2: e.tensor_tensor(out=t1[s2], in0=bank(b0 + 2), in1=sig[s2][0], op=ALU.mult),
                     reads=['ps%d' % (b0 + 2), 'sig%d_0' % s2], writes=['t1_%d' % s2])
                P.op('dve', lambda e, b0=b0, s2=s2: e.tensor_tensor(out=t2[s2], in0=bank(b0 + 3), in1=sig[s2][1], op=ALU.mult),
                     reads=['ps%d' % (b0 + 3), 'sig%d_1' % s2], writes=['t2_%d' % s2])
                P.op('dve', lambda e, s2=s2, dc=dc, m=m: e.tensor_tensor(out=mgT[:, dc, m * 512:(m + 1) * 512], in0=t1[s2], in1=t2[s2], op=ALU.add),
                     reads=['t1_%d' % s2, 't2_%d' % s2], writes=['mgT'])
        P.barrier()

        TA = Arena(big, T0 + 32768, TOT)
        Wout = TA.take([128, 8, 1024], BF16)
        xin2 = [TA.take([128, 1024], F32) for _ in range(2)]
        hmb = [TA.take([128, 1024], BF16) for _ in range(2)]
        junk4 = TA.take([128, 1024], BF16)
        scw = TA.take([128, 4 * NPAIR], F32)
        x1 = Arena(big, R1, R2).take([128, NPAIR, 1024], F32)
        hmT = Arena(big, R2, R3).take([128, 8, S // 2], BF16)
        uT = Arena(big, R3, T0).take([128, 8, S // 2], BF16)
        for half in range(2):
            P.dma('pool', Wout[:, :, half * 512:(half + 1) * 512], wsq(w_o, half * 512, 512), writes=['Wout%d' % half], dsem='wout%d' % half)
        P.dma('sp', gbcA, vb[:, VB_MLP:VB_MLP + 1024].partition_broadcast(128), writes=['gbcA'], dsem='gA')
        P.dma('sp', gbcB, vb[:, VB_FIN:VB_FIN + 1024].partition_broadcast(128), writes=['gbcB'], dsem='gB')
        xown = xl.rearrange("(q e p) d -> q e p d", e=2, p=128)
        pend4 = []
        for t in range(NPAIR):
            s2 = t % 2
            P.dma('sp', xin2[s2], xown[t, 0], writes=['xin2_%d' % s2], dsem='xin2_%d' % s2)
            for half in range(2):
                bk = 2 * s2 + half

                def pm(e, t=t, half=half, bk=bk):
                    for kc in range(8):
                        ins = e.matmul(bank(bk), lhsT=mgT[:, kc, t * 128:(t + 1) * 128], rhs=Wout[:, kc, half * 512:(half + 1) * 512],
                                       start=(kc == 0), stop=(kc == 7))
                    return ins
                P.op('pe', pm, reads=['mgT', 'Wout%d' % half], writes=['ps%d' % bk])
                P.op('dve', lambda e, t=t, half=half, bk=bk, s2=s2: e.tensor_tensor(
                    out=x1[:, t, half * 512:(half + 1) * 512], in0=bank(bk), in1=xin2[s2][:, half * 512:(half + 1) * 512], op=ALU.add),
                    reads=['ps%d' % bk, 'xin2_%d' % s2], writes=['x1_%d_%d' % (t, half)])
            xk = ['x1_%d_0' % t, 'x1_%d_1' % t]
            P.op('act', lambda e, t=t: e.activation(out=junk4, in_=x1[:, t, :], func=AF.Square, accum_out=scw[:, 4 * t:4 * t + 1]),
                 reads=xk, writes=['junk4', 'scw%d_a' % t])
            P.op('act', lambda e, t=t: e.activation(out=scw[:, 4 * t + 1:4 * t + 2], in_=scw[:, 4 * t:4 * t + 1], func=AF.Ln, scale=1.0 / D, bias=EPS),
                 reads=['scw%d_a' % t], writes=['scw%d_b' % t])
            P.op('act', lambda e, t=t: e.activation(out=scw[:, 4 * t + 2:4 * t + 3], in_=scw[:, 4 * t + 1:4 * t + 2], func=AF.Exp, scale=-0.5),
                 reads=['scw%d_b' % t], writes=['scw%d_c' % t])
            P.op('dve', lambda e, t=t, s2=s2: e.scalar_tensor_tensor(out=hmb[s2], in0=x1[:, t, :], scalar=scw[:, 4 * t + 2:4 * t + 3], in1=gbcA,
                                                                     op0=ALU.mult, op1=ALU.mult),
                 reads=xk + ['scw%d_c' % t, 'gbcA'], writes=['hmb%d' % s2])

            def tr4(t=t, s2=s2):
                def f(e):
                    for kc in range(8):
                        ins = e.transpose(out=bankb(4 + s2)[:, kc * 128:(kc + 1) * 128], in_=hmb[s2][:, kc * 128:(kc + 1) * 128], identity=identb)
                    return ins
                P.op('pe', f, reads=['hmb%d' % s2, 'identb'], writes=['ps%d' % (4 + s2)])
                evac(hmT[:, :, t * 128:(t + 1) * 128], bankb(4 + s2).rearrange("p (k t) -> p k t", k=8), ['ps%d' % (4 + s2)], ['hmT'])
            if pend4:
                pend4.pop(0)()
            pend4.append(tr4)
        while pend4:
            pend4.pop(0)()
        P.barrier()

        TA = Arena(big, T0, TOT)
        Wf = [TA.take([128, 8, 1024], BF16) for _ in range(3)]
        rl = [TA.take([128, 512], F32) for _ in range(2)]
        outt = [TA.take([128, 1024], F32) for _ in range(2)]
        junk5 = TA.take([128, 1024], BF16)
        scf = TA.take([128, 4 * NPAIR], F32)

        def load_wf(n):
            g = n // 2
            if n % 2 == 0:
                src = w_1[:, g * 1024:(g + 1) * 1024].rearrange("(kc p) f -> p kc f", p=128)
            else:
                src = w_2[g * 1024:(g + 1) * 1024, :].rearrange("(fc p) d -> p fc d", p=128)
            if n == 0:
                for fcl in range(8):
                    P.dma('pool', Wf[0][:, :, fcl * 128:(fcl + 1) * 128], src[:, :, fcl * 128:(fcl + 1) * 128],
                          writes=['Wf0c%d' % fcl], dsem='wf0c%d' % fcl)
                return
            extra = ['Wf0c%d' % f_ for f_ in range(8)] if n % 3 == 0 else []
            P.dma('pool', Wf[n % 3], src, writes=['Wf%d' % (n % 3)] + extra, dsem='wf%d' % (n % 3))

        for n in range(3):
            load_wf(n)
        out3 = out.rearrange("(t p) d -> t p d", p=128)
        j4 = 0
        for g in range(4):
            W1 = Wf[(2 * g) % 3]
            W2 = Wf[(2 * g + 1) % 3]
            k1, k2 = 'Wf%d' % ((2 * g) % 3), 'Wf%d' % ((2 * g + 1) % 3)
            for fcl in range(8):
                for m in range(4):
                    bk = j4 % 4
                    s2 = j4 % 2
                    j4 += 1

                    def pm(e, fcl=fcl, m=m, bk=bk, W1=W1):
                        for kc in range(8):
                            ins = e.matmul(bank(bk), lhsT=W1[:, kc, fcl * 128:(fcl + 1) * 128], rhs=hmT[:, kc, m * 512:(m + 1) * 512],
                                           start=(kc == 0), stop=(kc == 7))
                        return ins
                    P.op('pe', pm, reads=['hmT', ('Wf0c%d' % fcl) if g == 0 else k1], writes=['ps%d' % bk])
                    P.op('act', lambda e, bk=bk, s2=s2: e.activation(out=rl[s2], in_=bank(bk), func=AF.Relu), reads=['ps%d' % bk], writes=['rl%d' % s2])
                    P.op('dve', lambda e, s2=s2, fcl=fcl, m=m: e.tensor_tensor(out=uT[:, fcl, m * 512:(m + 1) * 512], in0=rl[s2], in1=rl[s2], op=ALU.mult),
                         reads=['rl%d' % s2], writes=['uT%d' % fcl])
            if 2 * g + 3 < 8:
                load_wf(2 * g + 3)
            for t in range(NPAIR):
                for half in range(2):
                    bk = 4 + (2 * t + half) % 4

                    def pm2(e, t=t, half=half, bk=bk, W2=W2):
                        for fcl in range(8):
                            ins = e.matmul(bank(bk), lhsT=uT[:, fcl, t * 128:(t + 1) * 128], rhs=W2[:, fcl, half * 512:(half + 1) * 512],
                                           start=(fcl == 0), stop=(fcl == 7))
                        return ins
                    P.op('pe', pm2, reads=['uT%d' % f_ for f_ in range(8)] + [k2], writes=['ps%d' % bk])
                    P.op('dve', lambda e, t=t, half=half, bk=bk: e.tensor_tensor(
                        out=x1[:, t, half * 512:(half + 1) * 512], in0=bank(bk), in1=x1[:, t, half * 512:(half + 1) * 512], op=ALU.add),
                        reads=['ps%d' % bk, 'x1_%d_%d' % (t, half)], writes=['x1_%d_%d' % (t, half)])
                if g == 3:
                    s2 = t % 2
                    xk = ['x1_%d_0' % t, 'x1_%d_1' % t]
                    P.op('act', lambda e, t=t: e.activation(out=junk5, in_=x1[:, t, :], func=AF.Square, accum_out=scf[:, 4 * t:4 * t + 1]),
                         reads=xk, writes=['junk5', 'scf%d_a' % t])
                    P.op('act', lambda e, t=t: e.activation(out=scf[:, 4 * t + 1:4 * t + 2], in_=scf[:, 4 * t:4 * t + 1], func=AF.Ln,
                                                            scale=1.0 / D, bias=EPS), reads=['scf%d_a' % t], writes=['scf%d_b' % t])
                    P.op('act', lambda e, t=t: e.activation(out=scf[:, 4 * t + 2:4 * t + 3], in_=scf[:, 4 * t + 1:4 * t + 2], func=AF.Exp, scale=-0.5),
                         reads=['scf%d_b' % t], writes=['scf%d_c' % t])
                    P.op('dve', lambda e, t=t, s2=s2: e.scalar_tensor_tensor(out=outt[s2], in0=x1[:, t, :], scalar=scf[:, 4 * t + 2:4 * t + 3], in1=gbcB,
                                                                             op0=ALU.mult, op1=ALU.mult),
                         reads=xk + ['scf%d_c' % t, 'gbcB'], writes=['outt%d' % s2])
                    P.dma('sp', out3[t], outt[s2], reads=['outt%d' % s2], dsem='out%d' % s2)
            if 2 * g + 4 < 8:
                load_wf(2 * g + 4)
        P.finish()
    return nc


def _tile_perm(c):
    perm = np.zeros(NT, np.int64)
    for p in range(NPAIR):
        perm[2 * p] = 2 * p + c
        perm[2 * p + 1] = 2 * p + (1 - c)
    return perm


def _make_pp(c, conv_w, conv_b, b_merge):
    pp = np.zeros((128, PP_N), np.float32)
    cw = conv_w.reshape(4, 8, 128)
    for ch in range(8):
        for j in range(4):
            pp[:, PP_CW + ch * 4 + j] = cw[j, ch]
    pp[:, PP_CB:PP_CB + 8] = conv_b.reshape(8, 128).T
    pp[:, PP_BM:PP_BM + 16] = b_merge.reshape(16, 128).T
    perm = _tile_perm(c)
    for i in range(NT):
        pp[:, PP_KPOS + i] = perm[i] * 128 + np.arange(128)
    for p in range(NPAIR):
        pp[:, PP_Q128 + p] = (2 * p + c) * 128 + 64
    for v in range(8):
        pp[:, PP_Q256 + v] = (4 * v + c) * 128 + 192
    for m in range(4):
        pp[:, PP_Q512 + m] = (8 * m + c) * 128 + 448
    pp[:, PP_FE] = float(c)
    pp[:, PP_FO] = float(1 - c)
    return pp


def _make_mask(c):
    mk = np.zeros((8, 128, 512), np.float32)
    kk = np.arange(128)[:, None]
    qq = np.arange(128)[None, :]
    for r in range(8):
        pk, par = r // 2, r % 2
        for u in range(4):
            blk = mk[r, :, u * 128:(u + 1) * 128]
            if pk < u:
                pass
            elif pk > u:
                blk[:] = NEG
            else:
                if par == 0:
                    blk[:] = np.where(kk <= qq, 0.0, NEG)
                else:
                    blk[:] = 0.0 if c == 1 else NEG
    return mk.transpose(1, 0, 2).reshape(128, 8 * 512).copy()


def _make_aug(c):
    perm = _tile_perm(c)
    kpos = (perm[:, None] * 128 + np.arange(128)[None, :]).reshape(-1)
    kaug = np.stack([kpos // 64, kpos % 64, np.ones_like(kpos), np.ones_like(kpos)]).astype(np.float32)
    qpos = np.array([(2 * p + c) * 128 + j for p in range(NPAIR) for j in range(128)])
    qaug = np.zeros((8, 4, S // 2), np.float32)
    for h in range(8):
        sl = 2.0 ** -(h + 1)
        qaug[h, 0] = 512.0 * sl
        qaug[h, 1] = 8.0 * sl
        qaug[h, 2] = -512.0 * sl * (qpos // 64)
        qaug[h, 3] = -8.0 * sl * (qpos % 64)
    return kaug, qaug


def _prep_inputs(x, norm_mix_g, w_in, b_gates, conv_w, conv_b, lam, da_norm_g, ml_norm_g,
                 b_merge, w_branch_a, w_branch_m, w_out, norm_mlp_g, w_ff1, w_ff2, norm_final_g):
    f = lambda a: np.ascontiguousarray(np.asarray(a, dtype=np.float32))
    x = f(x)
    vbrow = np.concatenate([f(norm_mix_g).reshape(-1), f(da_norm_g).reshape(-1), f(ml_norm_g).reshape(-1),
                            f(norm_mlp_g).reshape(-1), f(norm_final_g).reshape(-1), f(b_gates).reshape(-1),
                            f(lam).reshape(-1)]).reshape(1, VB_N)
    cst = np.concatenate([np.eye(128, dtype=np.float32), np.triu(np.ones((128, 128), np.float32)),
                          np.ones((128, 128), np.float32)], axis=1)
    shared = dict(w_in=f(w_in)[0], w_a=f(w_branch_a)[0], w_m=f(w_branch_m)[0], w_o=f(w_out)[0],
                  w_1=f(w_ff1)[0], w_2=f(w_ff2)[0], vb=vbrow, cst=cst)
    in_maps = []
    for core in range(8):
        b, c = core // 2, core % 2
        perm = _tile_perm(c)
        xl = x[b].reshape(NT, 128, D)[perm].reshape(S, D)
        m = dict(shared)
        m['xl'] = np.ascontiguousarray(xl)
        m['pp'] = _make_pp(c, f(conv_w)[0], f(conv_b)[0], f(b_merge)[0])
        m['mk'] = _make_mask(c)
        m['kaug'], m['qaug'] = _make_aug(c)
        in_maps.append(m)
    return in_maps


def kernel(**inputs):
    in_maps = _prep_inputs(**inputs)
    nc = build()
    res = run_bass_kernel_spmd(nc, in_maps, core_ids=list(range(8)))
    outp = np.zeros((4, S, D), np.float32)
    for core in range(8):
        b, c = core // 2, core % 2
        o = np.asarray(res.results[core]["out"]).reshape(NPAIR, 128, D)
        ov = outp[b].reshape(NT, 128, D)
        for p in range(NPAIR):
            ov[2 * p + c] = o[p]
    return outp
```
